# Optimizing a Trainium2 kernel written in Bass

```python
import math
import jax, jax.numpy as jnp
from jax import lax
import numpy as np

D_MODEL = 1024
BATCH = 4
SEQ = 4096
DEPTH = 2

RET_HEADS = 8
RET_DK = 64
RET_DV = 128
RET_CHUNK = 128
DIFF_HEADS = 8
DIFF_DK = 64
DIFF_DV = 2 * DIFF_DK
Q_BLOCK = 128
EPS = 1e-6

RET_QK = RET_HEADS * RET_DK
RET_V = RET_HEADS * RET_DV
DIFF_QK = DIFF_HEADS * 2 * DIFF_DK
DIFF_V = DIFF_HEADS * DIFF_DV
IN_SPLITS = (RET_QK, RET_QK, RET_V, RET_V, DIFF_QK, DIFF_QK, DIFF_V, DIFF_V, D_MODEL, D_MODEL)
N_IN = 2 * RET_QK + 2 * RET_V + 2 * DIFF_QK + 2 * DIFF_V + 2 * D_MODEL

kernel_name = "hybrid_retention_diffattn_gated_block"


def rms_norm(x, g):
    xf = x.astype(jnp.float32)
    xf = xf * lax.rsqrt(jnp.mean(xf * xf, axis=-1, keepdims=True) + EPS)
    return xf * g.astype(jnp.float32)


def retention(q, k, v):
    B, S, H, dk = q.shape
    dv = v.shape[-1]
    C = RET_CHUNK
    n = S // C
    log_gamma = jnp.log1p(-jnp.exp2(-5.0 - jnp.arange(H, dtype=jnp.float32)))
    pos = jnp.arange(C, dtype=jnp.float32)
    rel = pos[:, None] - pos[None, :]
    inner_decay = jnp.where(rel >= 0, jnp.exp(jnp.maximum(rel, 0.0)[None] * log_gamma[:, None, None]), 0.0)
    q_decay = jnp.exp((pos + 1.0)[None] * log_gamma[:, None])
    k_decay = jnp.exp((C - 1.0 - pos)[None] * log_gamma[:, None])
    chunk_decay = jnp.exp(C * log_gamma)

    def to_chunks(t):
        return t.astype(jnp.float32).reshape(B, n, C, H, t.shape[-1]).transpose(1, 0, 3, 2, 4)

    qc = to_chunks(q)
    kc = to_chunks(k) * (dk ** -0.5)
    vc = to_chunks(v)

    def step(state, inp):
        qi, ki, vi = inp
        scores = jnp.einsum('bhnd,bhmd->bhnm', qi, ki) * inner_decay
        out = (jnp.einsum('bhnm,bhme->bhne', scores, vi)
               + jnp.einsum('bhnd,bhde->bhne', qi, state) * q_decay[..., None])
        state = (state * chunk_decay[:, None, None]
                 + jnp.einsum('bhmd,bhme->bhde', ki * k_decay[..., None], vi))
        return state, out

    state0 = jnp.zeros((B, H, dk, dv), jnp.float32)
    _, out = lax.scan(step, state0, (qc, kc, vc))
    return out.transpose(1, 0, 3, 2, 4).reshape(B, S, H, dv)


def diff_attention(q, k, v, lam):
    B, S, H, _, dk = q.shape
    slopes = jnp.exp2(-8.0 * jnp.arange(1, H + 1, dtype=jnp.float32) / H)
    q = q.astype(jnp.float32) * (dk ** -0.5)
    k = k.astype(jnp.float32)
    v = v.astype(jnp.float32)
    outs = []
    for i in range(S // Q_BLOCK):
        start, end = i * Q_BLOCK, (i + 1) * Q_BLOCK
        qb = q[:, start:end]
        kb = k[:, :end]
        vb = v[:, :end]
        dist = (jnp.arange(start, end)[:, None] - jnp.arange(end)[None, :]).astype(jnp.float32)
        s = jnp.einsum('bqhjd,bkhjd->bhjqk', qb, kb) - slopes[None, :, None, None, None] * dist
        s = jnp.where(dist >= 0, s, -jnp.inf)
        p = jax.nn.softmax(s, axis=-1)
        a = p[:, :, 0] - lam * p[:, :, 1]
        outs.append(jnp.einsum('bhqk,bkhe->bqhe', a, vb))
    return jnp.concatenate(outs, axis=1)


def setup_inputs(seed: int = 0) -> dict:
    key = jax.random.key(seed)
    ks = jax.random.split(key, 15)
    f32 = jnp.float32
    n = lambda k, s: jax.random.normal(k, s, f32)
    return {
        "x": n(ks[0], (BATCH, SEQ, D_MODEL)),
        "norm_g": 1.0 + 0.02 * n(ks[1], (DEPTH, D_MODEL)),
        "w_in": n(ks[2], (DEPTH, D_MODEL, N_IN)) * D_MODEL ** -0.5,
        "ret_norm_g": 1.0 + 0.02 * n(ks[3], (DEPTH, RET_V)),
        "ret_w_o": n(ks[4], (DEPTH, RET_V, D_MODEL)) * RET_V ** -0.5,
        "diff_q_norm_g": 1.0 + 0.02 * n(ks[5], (DEPTH, DIFF_DK)),
        "diff_k_norm_g": 1.0 + 0.02 * n(ks[6], (DEPTH, DIFF_DK)),
        "diff_lq1": 0.1 * n(ks[7], (DEPTH, DIFF_DK)),
        "diff_lk1": 0.1 * n(ks[8], (DEPTH, DIFF_DK)),
        "diff_lq2": 0.1 * n(ks[9], (DEPTH, DIFF_DK)),
        "diff_lk2": 0.1 * n(ks[10], (DEPTH, DIFF_DK)),
        "diff_sub_norm_g": 1.0 + 0.02 * n(ks[11], (DEPTH, DIFF_V)),
        "diff_w_o": n(ks[12], (DEPTH, DIFF_V, D_MODEL)) * DIFF_V ** -0.5,
        "w_out": n(ks[13], (DEPTH, D_MODEL, D_MODEL)) * D_MODEL ** -0.5,
    }


def reference(x, norm_g, w_in, ret_norm_g, ret_w_o, diff_q_norm_g, diff_k_norm_g,
              diff_lq1, diff_lk1, diff_lq2, diff_lk2, diff_sub_norm_g, diff_w_o, w_out):
    B, S, _ = x.shape
    split_idx = [int(v) for v in np.cumsum(IN_SPLITS)[:-1]]
    for l in range(DEPTH):
        h = rms_norm(x, norm_g[l]).astype(x.dtype)
        z = jnp.einsum('bsd,dn->bsn', h, w_in[l])
        (q_r, k_r, v_r, gate_r, q_d, k_d, v_d, gate_d,
         mg_r, mg_d) = jnp.split(z, split_idx, axis=-1)

        o_r = retention(q_r.reshape(B, S, RET_HEADS, RET_DK),
                        k_r.reshape(B, S, RET_HEADS, RET_DK),
                        v_r.reshape(B, S, RET_HEADS, RET_DV))
        o_r = rms_norm(o_r, ret_norm_g[l].reshape(RET_HEADS, RET_DV)).reshape(B, S, RET_V)
        o_r = (o_r * jax.nn.silu(gate_r.astype(jnp.float32))).astype(x.dtype)
        y_r = jnp.einsum('bse,ed->bsd', o_r, ret_w_o[l])

        lam_init = 0.8 - 0.6 * math.exp(-0.3 * l)
        lam = (jnp.exp(jnp.sum(diff_lq1[l].astype(jnp.float32) * diff_lk1[l].astype(jnp.float32)))
               - jnp.exp(jnp.sum(diff_lq2[l].astype(jnp.float32) * diff_lk2[l].astype(jnp.float32)))
               + lam_init)
        qd = rms_norm(q_d.reshape(B, S, DIFF_HEADS, 2, DIFF_DK), diff_q_norm_g[l])
        kd = rms_norm(k_d.reshape(B, S, DIFF_HEADS, 2, DIFF_DK), diff_k_norm_g[l])
        o_d = diff_attention(qd, kd, v_d.reshape(B, S, DIFF_HEADS, DIFF_DV), lam)
        o_d = rms_norm(o_d, diff_sub_norm_g[l].reshape(DIFF_HEADS, DIFF_DV)) * (1.0 - lam_init)
        o_d = (o_d.reshape(B, S, DIFF_V) * jax.nn.silu(gate_d.astype(jnp.float32))).astype(x.dtype)
        y_d = jnp.einsum('bse,ed->bsd', o_d, diff_w_o[l])

        m = jax.nn.sigmoid(mg_r) * y_r + jax.nn.sigmoid(mg_d) * y_d
        x = (x + jnp.einsum('bsd,de->bse', m, w_out[l])).astype(x.dtype)
    return x
```

```python
import numpy as np, math
import concourse.bass as bass
import concourse.mybir as mybir

F32 = mybir.dt.float32
BF16 = mybir.dt.bfloat16
AF = mybir.ActivationFunctionType
ALU = mybir.AluOpType
AX = mybir.AxisListType


class Op:
    __slots__ = ("idx", "eng", "fn", "deps", "dma", "tok", "sig", "qprev", "cc")

    def __init__(self, idx, eng, fn, deps, dma, cc=False):
        self.idx = idx; self.eng = eng; self.fn = fn; self.deps = deps
        self.dma = dma; self.tok = None; self.sig = False; self.qprev = None; self.cc = cc


class Prog:
    ENGS = ("pe", "act", "dve", "pool", "sp")

    def __init__(self, nc, n_dma_sems=6):
        self.nc = nc
        self.ops = []
        self.lastw = {}
        self.readers = {}
        self.K = n_dma_sems
        self.last_eng = {}
        self.async_since_fence = []

    @staticmethod
    def _excl(k):
        return k == "psT" or (isinstance(k, tuple) and k[0] == "ps")

    def add(self, eng, fn, reads=(), writes=(), dma=False, cc=False, extra_deps=()):
        writes = list(writes) + [k for k in reads if self._excl(k)]
        reads = [k for k in reads if not self._excl(k)]
        deps = set()
        for k in reads:
            w = self.lastw.get(k)
            if w is not None:
                deps.add(w)
        for k in writes:
            w = self.lastw.get(k)
            if w is not None:
                deps.add(w)
            for r in self.readers.get(k, ()):
                deps.add(r)
        idx = len(self.ops)
        deps.update(extra_deps)
        deps.discard(idx)
        op = Op(idx, eng, fn, deps, dma or cc, cc)
        self.ops.append(op)
        if fn is not None:
            if dma or cc:
                self.async_since_fence.append(idx)
            else:
                self.last_eng[eng] = idx
        for k in writes:
            self.lastw[k] = idx
            self.readers[k] = []
        for k in reads:
            if k in writes:
                continue
            self.readers.setdefault(k, []).append(idx)
        return idx

    def pe(self, fn, reads=(), writes=()): return self.add("pe", fn, reads, writes)
    def act(self, fn, reads=(), writes=()): return self.add("act", fn, reads, writes)
    def dve(self, fn, reads=(), writes=()): return self.add("dve", fn, reads, writes)
    def pool(self, fn, reads=(), writes=()): return self.add("pool", fn, reads, writes)
    def dma(self, q, fn, reads=(), writes=()): return self.add(q, fn, reads, writes, dma=True)

    def coll(self, fn, reads=(), writes=()):
        return self.add("pool", fn, reads, writes, cc=True)

    def barrier_on(self, keys, engines):
        deps = set()
        for k in keys:
            w = self.lastw.get(k)
            if w is not None:
                deps.add(w)
            deps.update(self.readers.get(k, ()))
        for e in engines:
            self.add(e, None, extra_deps=deps)

    def fence(self):
        deps = set(self.last_eng.values()) | set(self.async_since_fence)
        for e in self.ENGS:
            self.add(e, None, extra_deps=deps)
        self.async_since_fence = []

    def emit(self):
        nc = self.nc
        ops = self.ops
        for op in ops:
            for d in op.deps:
                dop = ops[d]
                if dop.dma:
                    continue
                if dop.eng == "pe" and op.eng == "pe" and not op.dma:
                    continue
                dop.sig = True
        eng_sem = {}
        cnt = {}
        dma_sems = {}
        dma_cnt = {}
        dma_hist = {}
        for op in ops:
            if op.fn is None:
                continue
            if op.cc:
                op.tok = (nc.alloc_semaphore(f"cc_{op.idx}"), 1)
            elif op.dma:
                q = op.eng
                if q not in dma_sems:
                    dma_sems[q] = [nc.alloc_semaphore(f"dq_{q}_{i}") for i in range(self.K)]
                    dma_cnt[q] = 0
                    dma_hist[q] = []
                i = dma_cnt[q]
                dma_cnt[q] += 1
                op.tok = (dma_sems[q][i % self.K], 16 * (i // self.K + 1))
                if i >= self.K:
                    op.qprev = dma_hist[q][i - self.K]
                dma_hist[q].append(op.idx)
            elif op.sig:
                e = op.eng
                if e not in eng_sem:
                    eng_sem[e] = nc.alloc_semaphore(f"es_{e}")
                    cnt[e] = 0
                cnt[e] += 1
                op.tok = (eng_sem[e], cnt[e])
        by_eng = {e: [op for op in ops if op.eng == e] for e in self.ENGS}
        self.stats = {e: len(v) for e, v in by_eng.items()}
        nwaits = [0]

        def run_engine(e, lst):
            waited = {}
            for op in lst:
                need = []
                for d in sorted(op.deps):
                    dop = ops[d]
                    if dop.fn is None:
                        continue
                    if (not dop.dma) and dop.eng == "pe" and op.eng == "pe" and not op.dma:
                        continue
                    need.append(dop.tok)
                if op.qprev is not None:
                    need.append(ops[op.qprev].tok)
                best = {}
                for sem, val in need:
                    k = id(sem)
                    if k not in best or best[k][1] < val:
                        best[k] = (sem, val)
                for k, (sem, val) in best.items():
                    if waited.get(k, 0) >= val:
                        continue
                    e.wait_ge(sem, val)
                    nwaits[0] += 1
                    waited[k] = val
                if op.fn is None:
                    continue
                ins = op.fn(e)
                if op.tok is not None:
                    if op.cc:
                        ins.then_inc(op.tok[0])
                    else:
                        ins.then_inc(op.tok[0], 16 if op.dma else 1)

        with nc.Block() as block:
            for ename, battr in (("sp", "sync"), ("pe", "tensor"), ("act", "scalar"),
                                 ("dve", "vector"), ("pool", "gpsimd")):
                lst = by_eng[ename]
                if not lst:
                    continue
                getattr(block, battr)(lambda e, lst=lst: run_engine(e, lst))
        self.stats["waits"] = nwaits[0]
        return self.stats


import math
import numpy as np
import ml_dtypes

D = 1024
S = 4096
NT = 32
EPS = 1e-6
NH = 4
WA_COLS = 896
ACCW = 160
bf16 = ml_dtypes.bfloat16


def lam_init_of(l):
    return 0.8 - 0.6 * math.exp(-0.3 * l)


def host_consts(g):
    c = {}
    c["ident"] = np.eye(128, dtype=np.float32).astype(bf16)
    blk = np.zeros((128, 128), np.float32)
    blk[:64, :64] = 1.0 / 64
    blk[64:, 64:] = 1.0 / 64
    c["blk64"] = blk.astype(bf16)
    kk = np.arange(128)[:, None]
    qq = np.arange(128)[None, :]
    c["cmask"] = np.where(qq >= kk, 0.0, -30000.0).astype(np.float32).astype(bf16)
    t = np.arange(S)
    augk = np.stack([t % 128, t // 128, np.ones(S), np.ones(S)]).astype(np.float32)
    c["augk"] = augk.astype(bf16)
    augq = np.zeros((NH, 4, S), np.float32)
    dmask = np.zeros((128, NH, 128), np.float32)
    qdec = np.zeros((64, NH, 512), np.float32)
    kdec = np.zeros((128, 2 * NH), np.float32)
    for hh in range(NH):
        H = 4 * g + hh
        slope = 2.0 ** (-(H + 1))
        augq[hh, 0] = slope
        augq[hh, 1] = 128 * slope
        augq[hh, 2] = -slope * (t % 128)
        augq[hh, 3] = -128 * slope * (t // 128)
        lg = math.log1p(-2.0 ** (-5 - H))
        rel = qq - kk
        dmask[:, hh, :] = np.where(rel >= 0, np.exp(np.maximum(rel, 0) * lg), 0.0)
        qdec[:, hh, :] = np.exp(((np.arange(512) % 128) + 1.0) * lg)[None, :]
        kdec[:, hh] = np.exp((127.0 - np.arange(128)) * lg) * 0.125
        kdec[:, NH + hh] = chunk_decay(H)
    assert np.array_equal(augq.astype(bf16).astype(np.float32), augq)
    assert np.array_equal(augk.astype(bf16).astype(np.float32), augk)
    c["augq"] = augq.astype(bf16)
    c["dmask"] = dmask
    c["qdec"] = qdec
    c["kdec"] = kdec
    return c


def chunk_decay(H):
    return math.exp(128.0 * math.log1p(-2.0 ** (-5 - H)))


def host_weights_A(inp, l, g):
    w = inp["w_in"][l]
    out = np.empty((NH, D, WA_COLS), np.float32)
    for hh in range(NH):
        H = 4 * g + hh
        out[hh, :, 0:128] = w[:, 3072 + H * 128: 3072 + (H + 1) * 128]
        out[hh, :, 128:256] = w[:, 4096 + H * 128: 4096 + (H + 1) * 128]
        out[hh, :, 256:320] = w[:, 0 + H * 64: (H + 1) * 64]
        out[hh, :, 320:384] = w[:, 512 + H * 64: 512 + (H + 1) * 64]
        out[hh, :, 384:512] = w[:, 5120 + H * 128: 5120 + (H + 1) * 128]
        out[hh, :, 512:640] = w[:, 6144 + H * 128: 6144 + (H + 1) * 128]
        out[hh, :, 640:768] = w[:, 1024 + H * 128: 1024 + (H + 1) * 128]
        out[hh, :, 768:896] = w[:, 2048 + H * 128: 2048 + (H + 1) * 128]
    wA = np.ascontiguousarray(out.reshape(NH, 8, 128, WA_COLS))
    sp = np.empty((128, 8 + 2 + 256), np.float32)
    sp[:, 0:8] = inp["norm_g"][l].reshape(8, 128).T
    sp[:, 8] = np.tile(inp["diff_q_norm_g"][l], 2)
    sp[:, 9] = np.tile(inp["diff_k_norm_g"][l], 2)
    lv = np.concatenate([inp["diff_lq1"][l], inp["diff_lk1"][l], inp["diff_lq2"][l], inp["diff_lk2"][l]])
    sp[:, 10:] = np.broadcast_to(lv[None, :], (128, 256))
    return wA, sp


class Ctx:
    pass


def alloc_common(P, nc, cd):
    C = Ctx()
    C.ps = [nc.alloc_psum_tensor(f"ps{i}", [128, 512], F32) for i in range(7)]
    C.psT = nc.alloc_psum_tensor("psT", [128, 1024], BF16)
    C.ident = nc.alloc_sbuf_tensor("c_ident", [128, 128], BF16)
    C.blk64 = nc.alloc_sbuf_tensor("c_blk64", [128, 128], BF16)
    C.cmask = nc.alloc_sbuf_tensor("c_cmask", [128, 128], BF16)
    C.dmask = nc.alloc_sbuf_tensor("c_dmask", [128, NH, 128], F32)
    C.qdec = nc.alloc_sbuf_tensor("c_qdec", [64, NH, 512], F32)
    C.kdec = nc.alloc_sbuf_tensor("c_kdec", [128, 2 * NH], F32)
    for nm in ("ident", "blk64", "cmask", "dmask", "qdec", "kdec"):
        t = getattr(C, nm)
        P.dma("sp", lambda e, t=t, nm=nm: e.dma_start(out=t[:], in_=cd[nm]), writes=[nm])
    return C


def alloc_N(nc, tag="N", al=None):
    al = al or nc.alloc_sbuf_tensor
    N = Ctx()
    N.xt = [al(f"{tag}_xt{i}", [128, D], F32) for i in range(2)]
    N.sq = al(f"{tag}_sq", [128, D], BF16)
    N.xn = [al(f"{tag}_xn{i}", [128, D], BF16) for i in range(2)]
    N.ss = [al(f"{tag}_ss{i}", [128, 1], F32) for i in range(2)]
    N.rs = [al(f"{tag}_rs{i}", [128, 1], F32) for i in range(2)]
    return N


def norm_tile(P, C, N, b, src_key, dst, dst_key):
    norm_tile_a(P, C, N, b, src_key)
    norm_tile_t(P, C, N, b, dst, dst_key)


def norm_tile_a(P, C, N, b, src_key, sq_eng="act", stats_only=False):
    if sq_eng == "act":
        P.act(lambda e: e.activation(out=N.xn[b][:], in_=N.xt[b][:], func=AF.Square, accum_out=N.ss[b][:]),
              reads=[src_key], writes=[("N", "xn", b), ("N", "ss", b)])
    else:
        P.dve(lambda e: e.scalar_tensor_tensor(out=N.xn[b][:], in0=N.xt[b][:], scalar=1.0, in1=N.xt[b][:], op0=ALU.mult, op1=ALU.mult,
                                               accum_out=N.ss[b][:]),
              reads=[src_key], writes=[("N", "xn", b), ("N", "ss", b)])
    P.act(lambda e: e.activation(out=N.ss[b][:], in_=N.ss[b][:], func=AF.Ln, scale=1.0 / D, bias=EPS),
          reads=[("N", "ss", b)], writes=[("N", "ss", b)])
    P.act(lambda e: e.activation(out=N.rs[b][:], in_=N.ss[b][:], func=AF.Exp, scale=-0.5),
          reads=[("N", "ss", b)], writes=[("N", "rs", b)])
    if not stats_only:
        norm_tile_scale(P, N, b, src_key)


def norm_tile_scale(P, N, b, src_key):
    P.dve(lambda e: e.tensor_scalar(out=N.xn[b][:], in0=N.xt[b][:], scalar1=N.rs[b][:], scalar2=None, op0=ALU.mult),
          reads=[src_key, ("N", "rs", b)], writes=[("N", "xn", b)])


def norm_tile_t(P, C, N, b, dst, dst_key, eng="act"):
    for c in range(8):
        P.pe(lambda e, c=c: e.transpose(out=C.psT[:, c * 128:(c + 1) * 128], in_=N.xn[b][:, c * 128:(c + 1) * 128],
                                        identity=C.ident[:]),
             reads=[("N", "xn", b), "ident"], writes=["psT"])
    if eng == "act":
        P.act(lambda e: e.activation(out=dst, in_=C.psT[:].rearrange("p (c t) -> p c t", c=8), func=AF.Copy),
              reads=["psT"], writes=[dst_key])
    else:
        P.dve(lambda e: e.tensor_copy(out=dst, in_=C.psT[:].rearrange("p (c t) -> p c t", c=8)),
              reads=["psT"], writes=[dst_key])


def alloc_A(nc, al=None):
    al = al or nc.alloc_sbuf_tensor
    A = Ctx()
    A.hT = al("sb_hT", [128, 8, S], BF16)
    A.wbf = al("wbf", [128, 8, WA_COLS], BF16)
    A.wstg = [al(f"wstg{i}", [128, 1, WA_COLS], F32) for i in range(2)]
    A.sp = al("spar", [128, 8 + 2 + 256], F32)
    A.sgt = [al(f"sgt{i}", [128, 2, 128], F32) for i in range(2)]
    A.sqb2 = [al(f"sqb{i}", [128, 512], BF16) for i in range(2)]
    A.dead_keys = ([("hT", T) for T in range(8)] + [("wbf", c) for c in range(8)] + [("wstg", i) for i in range(2)]
                   + ["spar"] + [("sgt", i) for i in range(2)] + [("sqb", i) for i in range(2)])
    A.accS = al("accS", [128, 8, 129], F32)
    A.QT = [al(f"QT{j}", [128, S], BF16) for j in range(2)]
    A.KT = [al(f"KT{j}", [128, S], BF16) for j in range(2)]
    A.RQ = al("RQ", [128, S], BF16)
    A.RK = al("RK", [128, S], BF16)
    A.KrT = al("KrT", [128, NT, 64], BF16)
    A.V2 = al("V2", [128, NT, 2, 129], BF16)
    A.G2 = al("G2", [128, NT, 2, 128], BF16)
    A.PT = [[al(f"PT{j}{b}", [128, 512], BF16) for b in range(2)] for j in range(2)]
    A.lnv = al("lnv", [128, 512], F32)
    A.rstd = al("rstd", [128, 512], F32)
    A.Sf = [al(f"Sf{i}", [64, 128], F32) for i in range(2)]
    A.sm = al("sm", [128, 16], F32)
    A.sm2 = al("sm2", [128, 40], F32)
    A.lam = al("lamt", [128, 8], F32)
    A.lprod = al("lprod", [128, 2, 64], F32)
    A.t1 = al("t1", [128, 128], F32)
    A.oS = al("oS", [128, 4, 128], F32)
    A.sqj = al("sqj", [128, 128], F32)
    A.ob4 = al("ob4", [128, 4, 128], BF16)
    A.oTst = [al(f"oTst{i}", [128, 512], BF16) for i in range(2)]
    A.n_oTst = 0
    A.out_keys = []
    return A


def emit_A_weights(P, A, wA_dram, hh):
    for c in range(8):
        sb = c % 2
        P.dma("sp", lambda e, c=c, sb=sb, hh=hh: e.dma_start(out=A.wstg[sb][:, 0, :], in_=wA_dram[hh, c, :, :]),
              writes=[("wstg", sb)])
        P.pool(lambda e, c=c, sb=sb: e.tensor_scalar(out=A.wbf[:, c, :], in0=A.wstg[sb][:, 0, :],
                                                     scalar1=A.sp[:, c:c + 1], scalar2=1.0,
                                                     op0=ALU.mult, op1=ALU.mult),
               reads=[("wstg", sb), "spar"], writes=[("wbf", c)])


def stage_A_pre_thunks(P, A, wA_dram, sp_dram):
    out = [lambda: P.dma("sp", lambda e: e.dma_start(out=A.sp[:], in_=sp_dram), writes=["spar"])]

    def piece(c):
        sb = c % 2
        P.dma("sp", lambda e: e.dma_start(out=A.wstg[sb][:, 0, :], in_=wA_dram[0, c, :, :]), writes=[("wstg", sb)])
        P.pool(lambda e: e.tensor_scalar(out=A.wbf[:, c, :], in0=A.wstg[sb][:, 0, :], scalar1=A.sp[:, c:c + 1], scalar2=1.0,
                                         op0=ALU.mult, op1=ALU.mult),
               reads=[("wstg", sb), "spar"], writes=[("wbf", c)])
    for c in range(8):
        out.append(lambda c=c: piece(c))
    return out


def stage_A_pre(P, A, wA_dram, sp_dram):
    P.dma("sp", lambda e: e.dma_start(out=A.sp[:], in_=sp_dram), writes=["spar"])
    emit_A_weights(P, A, wA_dram, 0)


def stage_A(P, nc, C, A, l, g, hT_dram, wA_dram, sp_dram, augk_dram, augq_dram, oT_dram, first, oT_dst=None, head_done=None, last_proj_done=None, step_hook=None, pre_done=False):
    ps = C.ps
    lam_init = lam_init_of(l)
    if oT_dst is None:
        oT_dst = lambda br, hh, tsl: oT_dram[br, hh, :, tsl]
    if hT_dram is None:
        pass
    elif callable(hT_dram):
        for r in range(2):
            for T in range(4):
                tsl = slice(r * 2048 + T * 512, r * 2048 + (T + 1) * 512)
                P.dma("sp", lambda e, r=r, T=T, tsl=tsl: e.dma_start(out=A.hT[:, :, tsl], in_=hT_dram(r, T)),
                      writes=[("hT", r * 4 + T)])
    else:
        for r in range(2):
            for c in range(8):
                P.dma("sp", lambda e, r=r, c=c: e.dma_start(out=A.hT[:, c, r * 2048:(r + 1) * 2048],
                                                           in_=hT_dram[r, c * 128:(c + 1) * 128, :]),
                      writes=[("hT", r * 4 + i) for i in range(4)])
    if not pre_done:
        stage_A_pre(P, A, wA_dram, sp_dram)
    if first:
        P.pool(lambda e: e.memset(A.V2[:, :, :, 128:129], 1.0), writes=["V2ones"])
        for j in range(2):
            P.pool(lambda e, j=j: e.memset(A.KT[j][64:128, :], 0.0), writes=[("KTaug", j)])
            P.pool(lambda e, j=j: e.memset(A.QT[j][64:128, :], 0.0), writes=[("QTaug", j)])
            P.dma("sp", lambda e, j=j: e.dma_start(out=A.KT[j][64:68, :], in_=augk_dram), writes=[("KTaug", j)])
    P.dve(lambda e: e.tensor_scalar(out=A.sm[:, 14:15], in0=A.sp[:, 8:9], scalar1=0.125, scalar2=None, op0=ALU.mult),
          reads=["spar"], writes=["gq"])
    P.dve(lambda e: e.tensor_copy(out=A.sm[:, 15:16], in_=A.sp[:, 9:10]), reads=["spar"], writes=["gk"])
    lv = A.sp[:, 10:266].rearrange("p (a k) -> p a k", a=4)
    P.dve(lambda e: e.tensor_tensor(out=A.lprod[:], in0=lv[:, 0:4:2, :], in1=lv[:, 1:4:2, :], op=ALU.mult),
          reads=["spar"], writes=["lprod"])
    P.dve(lambda e: e.tensor_reduce(out=A.lam[:, 0:2], in_=A.lprod[:], axis=AX.X, op=ALU.add),
          reads=["lprod"], writes=["lam01"])
    P.act(lambda e: e.activation(out=A.lam[:, 2:4], in_=A.lam[:, 0:2], func=AF.Exp), reads=["lam01"], writes=["lam23"])
    P.dve(lambda e: e.tensor_tensor(out=A.lam[:, 4:5], in0=A.lam[:, 3:4], in1=A.lam[:, 2:3], op=ALU.subtract),
          reads=["lam23"], writes=["lam4"])
    P.dve(lambda e: e.tensor_scalar(out=A.lam[:, 5:6], in0=A.lam[:, 4:5], scalar1=-lam_init, scalar2=None, op0=ALU.add),
          reads=["lam4"], writes=["neglam"])
    neglam = A.lam[:, 5:6]

    hT_keys = lambda T: [("hT", T)]

    stop = getattr(A, "stop", 99)
    nheads = getattr(A, "nheads", NH)

    def emit_weights(hh):
        emit_A_weights(P, A, wA_dram, hh)

    tail_thunks = []
    tail_head = [None]
    for hh in range(nheads):
        cdec = C.kdec[0:64, NH + hh:NH + hh + 1]
        wkeys = [("wbf", c) for c in range(8)]
        for j in range(2):
            P.dma("sp", lambda e, j=j, hh=hh: e.dma_start(out=A.QT[j][64:68, :], in_=augq_dram[hh]),
                  writes=[("QTaug", j)])
        if stop <= 0:
            continue
        ring = (0, 1, 2, 4)
        statb = (3, 5)
        groups = [(T, gi) for T in range(8) for gi in range(3)]

        def f_a(n):
            T, gi = groups[n]
            b = ring[n % 4]
            tok = slice(T * 512, (T + 1) * 512)
            for c in range(8):
                P.pe(lambda e, b=b, c=c, gi=gi, tok=tok: e.matmul(ps[b][:], lhsT=A.wbf[:, c, gi * 128:(gi + 1) * 128],
                                                                 rhs=A.hT[:, c, tok], start=(c == 0), stop=(c == 7)),
                     reads=wkeys + hT_keys(T), writes=[("ps", b)])

        def f_b(n, hh=hh):
            T, gi = groups[n]
            b = ring[n % 4]
            tok = slice(T * 512, (T + 1) * 512)
            if gi < 2:
                P.act(lambda e, b=b, n=n: e.activation(out=A.sqb2[n % 2][:], in_=ps[b][:], func=AF.Square),
                      reads=[("ps", b)], writes=[("sqb", n % 2)])
            else:
                P.dve(lambda e, b=b, tok=tok: e.tensor_copy(out=A.RQ[0:64, tok], in_=ps[b][0:64, :]),
                      reads=[("ps", b)], writes=[("RQ", T)])
                P.dve(lambda e, b=b, tok=tok, hh=hh: e.tensor_tensor(out=A.RQ[64:128, tok], in0=ps[b][0:64, :],
                                                                    in1=C.qdec[:, hh, :], op=ALU.mult),
                      reads=[("ps", b), "qdec"], writes=[("RQd", T)])
                P.dve(lambda e, b=b, tok=tok: e.tensor_scalar(out=A.RK[0:64, tok], in0=ps[b][64:128, :], scalar1=0.125, scalar2=None,
                                                             op0=ALU.mult),
                      reads=[("ps", b)], writes=[("RK", T)])

        def f_cd(n):
            T, gi = groups[n]
            if gi >= 2:
                return
            b = ring[n % 4]
            sbk = statb[n % 2]
            tok = slice(T * 512, (T + 1) * 512)
            dst = A.QT if gi == 0 else A.KT
            dkey = "QT" if gi == 0 else "KT"
            gcol = A.sm[:, 14:15] if gi == 0 else A.sm[:, 15:16]
            gkey = "gq" if gi == 0 else "gk"
            P.pe(lambda e, sbk=sbk, n=n: e.matmul(ps[sbk][:], lhsT=C.blk64[:], rhs=A.sqb2[n % 2][:], start=True, stop=True),
                 reads=[("sqb", n % 2), "blk64"], writes=[("ps", sbk)])
            P.act(lambda e, sbk=sbk: e.activation(out=A.lnv[:], in_=ps[sbk][:], func=AF.Ln, bias=EPS),
                  reads=[("ps", sbk)], writes=["lnv"])
            P.act(lambda e: e.activation(out=A.rstd[:], in_=A.lnv[:], func=AF.Exp, scale=-0.5), reads=["lnv"], writes=["rstd"])
            for j in range(2):
                pr = slice(64 * j, 64 * j + 64)
                P.dve(lambda e, b=b, j=j, pr=pr, dst=dst, gcol=gcol, tok=tok: e.scalar_tensor_tensor(
                    out=dst[j][0:64, tok], in0=ps[b][pr, :], scalar=gcol[pr, :], in1=A.rstd[pr, :],
                    op0=ALU.mult, op1=ALU.mult),
                    reads=[("ps", b), "rstd", gkey], writes=[(dkey, j, T)])

        for n in range(len(groups)):
            f_a(n)
            f_b(n)
            if n >= 1:
                f_cd(n - 1)
            if n == 2 and tail_thunks:
                for th in tail_thunks:
                    th()
                tail_thunks[:] = []
                if head_done is not None:
                    head_done(tail_head[0])
        f_cd(len(groups) - 1)
        if stop <= 1:
            continue
        P.dve(lambda e: e.memset(A.Sf[0][:], 0.0), writes=[("Sf", 0)])
        P.pool(lambda e: e.memset(A.RK[64:128, 0:128], 0.0), writes=[("Sb", 0)])
        n_state_done = [0]

        def state_steps(i0, i1, cdec=cdec):
            for i in range(i0, i1):
                b = (i // 4) % 2
                sl = slice((i % 4) * 128, (i % 4 + 1) * 128)
                if i % 4 == 3:
                    P.pe(lambda e, b=b, sl=sl, i=i: e.matmul(ps[b][0:64, sl], lhsT=A.KrT[:, i, :], rhs=A.V2[:, i, 1, 0:128],
                                                            start=True, stop=True),
                         reads=[("KrT", i // 4), ("V2", i)], writes=[("ps", b)])
                    continue
                P.pe(lambda e, b=b, sl=sl, i=i: e.matmul(ps[b][:, sl], lhsT=A.KrT[:, i:i + 2, :].rearrange("p a d -> p (a d)"),
                                                        rhs=A.V2[:, i, 1, 0:128], start=True, stop=True),
                     reads=[("KrT", i // 4), ("KrT", (i + 1) // 4), ("V2", i)], writes=[("ps", b)])
            for i in range(i0, i1):
                b = (i // 4) % 2
                sl = slice((i % 4) * 128, (i % 4 + 1) * 128)
                P.dve(lambda e, b=b, sl=sl, i=i: e.scalar_tensor_tensor(out=A.Sf[(i + 1) % 2][:], in0=A.Sf[i % 2][:], scalar=cdec,
                                                                       in1=ps[b][0:64, sl], op0=ALU.mult, op1=ALU.add),
                      reads=[("ps", b), ("Sf", i % 2), "kdec"], writes=[("Sf", (i + 1) % 2)])
                P.act(lambda e, i=i: e.activation(out=A.RK[64:128, (i + 1) * 128:(i + 2) * 128], in_=A.Sf[(i + 1) % 2][:], func=AF.Copy),
                      reads=[("Sf", (i + 1) % 2)], writes=[("Sb", i + 1)])
            n_state_done[0] = i1

        def rbanks(g):
            return (2, 3) if g < 7 else (2 + g % 2, 4 + g % 2)

        oSb = [A.lnv[:].rearrange("p (a e) -> p a e", a=4), A.rstd[:].rearrange("p (a e) -> p a e", a=4)]
        oSk = ["lnv", "rstd"]
        scT4 = [A.PT[0][0][:].rearrange("p (a e) -> p a e", a=4), A.PT[0][1][:].rearrange("p (a e) -> p a e", a=4)]
        scTk = [("PT", 0, 0), ("PT", 0, 1)]
        ob4r = [A.PT[1][0][:].rearrange("p (a e) -> p a e", a=4), A.PT[1][1][:].rearrange("p (a e) -> p a e", a=4)]
        ob4k = [("PT", 1, 0), ("PT", 1, 1)]
        dm4 = C.dmask[:, hh, :].unsqueeze(1).broadcast_to([128, 4, 128])

        def r_s1(g):
            bS = rbanks(g)[0]
            for j in range(4):
                i = 4 * g + j
                ch = slice(i * 128, (i + 1) * 128)
                P.pe(lambda e, bS=bS, j=j, ch=ch: e.matmul(ps[bS][:, j * 128:(j + 1) * 128], lhsT=A.RK[0:64, ch],
                                                          rhs=A.RQ[0:64, ch], start=True, stop=True),
                     reads=[("RK", g), ("RQ", g)], writes=[("ps", bS)])

        def r_s2(g):
            bS = rbanks(g)[0]
            P.dve(lambda e, bS=bS, g=g, dm4=dm4: e.tensor_tensor(out=scT4[g % 2], in0=ps[bS][:].rearrange("p (a e) -> p a e", a=4),
                                                                 in1=dm4, op=ALU.mult),
                  reads=[("ps", bS), "dmask"], writes=[scTk[g % 2]])

        def r_s3(g):
            bO = rbanks(g)[1]
            for j in range(4):
                i = 4 * g + j
                P.pe(lambda e, bO=bO, j=j, g=g, i=i: e.matmul(ps[bO][:, j * 128:(j + 1) * 128], lhsT=scT4[g % 2][:, j, :],
                                                             rhs=A.V2[:, i, 1, 0:128], start=(j == 0), stop=False,
                                                             skip_group_check=True),
                     reads=[scTk[g % 2], ("V2", i)], writes=[("ps", bO)])
            for j in range(4):
                i = 4 * g + j
                ch = slice(i * 128, (i + 1) * 128)
                P.pe(lambda e, bO=bO, j=j, ch=ch: e.matmul(ps[bO][:, j * 128:(j + 1) * 128], lhsT=A.RQ[64:128, ch],
                                                          rhs=A.RK[64:128, ch], start=False, stop=True, skip_group_check=True),
                     reads=[("RQd", g), ("Sb", i)], writes=[("ps", bO)])

        def r_s4(g):
            bO = rbanks(g)[1]
            P.dve(lambda e, bO=bO, g=g: e.tensor_copy(out=oSb[g % 2], in_=ps[bO][:].rearrange("p (a e) -> p a e", a=4)),
                  reads=[("ps", bO)], writes=[oSk[g % 2]])
            finalize_a(P, A, oSb[g % 2], oSk[g % 2], 12 * (g % 2))

        def r_s5(g):
            finalize_b(P, C, A, oSb[g % 2], oSk[g % 2], 12 * (g % 2), ob4r[g % 2], ob4k[g % 2],
                       gate=lambda j, g=g: A.G2[:, 4 * g + j, 1, :], gate_keys=[("G2", 4 * g + j) for j in range(4)], split=True)

        def r_s6(g):
            finalize_b2(P, C, ob4r[g % 2], ob4k[g % 2])
            flush_oT(P, C, A, oT_dst(0, hh, slice(g * 512, (g + 1) * 512)), eng="act")

        rsched = {}
        for g in range(8):
            t0 = 4 * g + 3
            for dt, fns in ((0, (r_s1, r_s2)), (2, (r_s3, r_s4)), (4, (r_s5,)), (6, (r_s6,))):
                for fn in fns:
                    rsched.setdefault(t0 + dt, []).append((fn, g))
        for t in range(NT):
            b = 4 + t % 2
            T = t // 4
            ts_ = slice(t * 128, (t + 1) * 128)
            for c in range(8):
                P.pe(lambda e, b=b, c=c, ts_=ts_: e.matmul(ps[b][:], lhsT=A.hT[:, c, ts_], rhs=A.wbf[:, c, 384:896],
                                                          start=(c == 0), stop=(c == 7)),
                     reads=wkeys + hT_keys(T), writes=[("ps", b)])
            pv = lambda b: ps[b][:].rearrange("p (a b e) -> p a b e", a=2, b=2)
            P.act(lambda e, b=b, t=t: e.activation(out=A.V2[:, t, :, 0:128], in_=pv(b)[:, :, 0, :], func=AF.Copy),
                  reads=[("ps", b)], writes=[("V2", t)])
            sgb = t % 2
            P.act(lambda e, b=b, sgb=sgb: e.activation(out=A.sgt[sgb][:], in_=pv(b)[:, :, 1, :], func=AF.Sigmoid),
                  reads=[("ps", b)], writes=[("sgt", sgb)])
            P.dve(lambda e, b=b, t=t, sgb=sgb: e.tensor_tensor(out=A.G2[:, t, :, :], in0=pv(b)[:, :, 1, :], in1=A.sgt[sgb][:],
                                                              op=ALU.mult),
                  reads=[("ps", b), ("sgt", sgb)], writes=[("G2", t)])
            sl = t % 4
            for c in range(8):
                P.pe(lambda e, c=c, ts_=ts_, sl=sl: e.matmul(ps[6][:, sl * 64:(sl + 1) * 64], lhsT=A.hT[:, c, ts_],
                                                            rhs=A.wbf[:, c, 320:384], start=(c == 0), stop=(c == 7)),
                     reads=wkeys + hT_keys(T), writes=[("ps", 6)])
            if sl == 3:
                P.dve(lambda e, t=t, hh=hh: e.tensor_scalar(out=A.KrT[:, t - 3:t + 1, :],
                                                           in0=ps[6][:, 0:256].rearrange("p (s d) -> p s d", s=4),
                                                           scalar1=C.kdec[:, hh:hh + 1], scalar2=None, op0=ALU.mult),
                      reads=[("ps", 6), "kdec"], writes=[("KrT", t // 4)])
            if t >= 4 and t % 4 == 0:
                state_steps(t - 4, t)
            for fn, g in rsched.pop(t, []):
                fn(g)
        if hh + 1 < nheads:
            emit_weights(hh + 1)
        elif last_proj_done is not None:
            last_proj_done()
        state_steps(n_state_done[0], min(n_state_done[0] + 4, NT - 1))
        state_steps(n_state_done[0], NT - 1)
        for tt in sorted(rsched):
            for fn, g in rsched[tt]:
                fn(g)
        rsched.clear()
        if stop <= 4:
            continue
        islast = hh == nheads - 1
        tail_thunks[:] = attention_head(P, C, A, hh, neglam, oT_dst, step_hook if islast else None, defer_tail=not islast)
        tail_head[0] = hh
        if islast and head_done is not None:
            head_done(hh)


def attention_head(P, C, A, hh, neglam, oT_dst, step_hook=None, defer_tail=False):
    ps = C.ps
    steps = [(Tq, kt) for Tq in range(8) for kt in range(4 * Tq + 4)]

    def geom(Tq, kt):
        diag = kt >= 4 * Tq
        off = (kt - 4 * Tq) * 128 if diag else 0
        return diag, off, 512 - off

    def emit_qk(s):
        Tq, kt = steps[s]
        diag, off, N = geom(Tq, kt)
        q0 = Tq * 512
        sb = s % 2
        for j in range(2):
            bq = j * 2 + sb
            P.pe(lambda e, bq=bq, j=j, kt=kt, N=N, off=off, q0=q0, diag=diag: e.matmul(
                ps[bq][:, 0:N], lhsT=A.KT[j][:, kt * 128:(kt + 1) * 128],
                rhs=A.QT[j][:, q0 + off:q0 + 512], start=True, stop=not diag),
                reads=[("KT", j, kt // 4), ("KTaug", j), ("QTaug", j), ("QT", j, Tq)], writes=[("ps", bq)])
            if diag:
                P.pe(lambda e, bq=bq: e.matmul(ps[bq][:, 0:128], lhsT=C.ident[:], rhs=C.cmask[:], start=False, stop=True),
                     reads=["ident", "cmask"], writes=[("ps", bq)])
            P.act(lambda e, bq=bq, j=j, sb=sb, N=N: e.activation(out=A.PT[j][sb][:, 0:N], in_=ps[bq][:, 0:N], func=AF.Exp),
                  reads=[("ps", bq)], writes=[("PT", j, sb)])

    started = set()

    def emit_pv(s):
        Tq, kt = steps[s]
        diag, off, N = geom(Tq, kt)
        sb = s % 2
        if kt == 0:
            started.clear()
        for j in range(2):
            for qs in range(off // 128, 4):
                a = qs * 2 + j
                bank = 4 + a // 3
                col = (a % 3) * ACCW
                st = bank not in started
                started.add(bank)
                P.pe(lambda e, bank=bank, col=col, j=j, sb=sb, qs=qs, off=off, kt=kt, st=st, Tq=Tq: e.matmul(
                    ps[bank][:, col:col + 129], lhsT=A.PT[j][sb][:, qs * 128 - off:qs * 128 - off + 128],
                    rhs=A.V2[:, kt, 0, :], start=st, stop=(kt == 4 * Tq + qs), skip_group_check=True),
                    reads=[("PT", j, sb), ("V2", kt), "V2ones"], writes=[("ps", bank)])

    def emit_acc_copy(bk):
        na = 3 if bk < 2 else 2
        P.dve(lambda e, bk=bk, na=na: e.tensor_copy(
            out=A.accS[:, 3 * bk:3 * bk + na, :],
            in_=ps[4 + bk][:, 0:na * ACCW].rearrange("p (a w) -> p a w", w=ACCW)[:, :, 0:129]),
            reads=[("ps", 4 + bk)], writes=[("accS", bk)])

    def emit_final(Tq):
        q0 = Tq * 512
        allacc = [("accS", bk) for bk in range(3)]
        P.dve(lambda e: e.reciprocal(out=A.sm[:, 0:8], in_=A.accS[:, :, 128]), reads=allacc, writes=["rc8"])
        P.dve(lambda e: e.tensor_scalar(out=A.sm[:, 8:12], in0=A.sm[:, 1:8:2], scalar1=neglam, scalar2=None, op0=ALU.mult),
              reads=["rc8", "neglam"], writes=["rcl"])
        for qs in range(4):
            a0, a1 = qs * 2, qs * 2 + 1
            P.dve(lambda e, a1=a1, qs=qs: e.tensor_scalar(out=A.t1[:], in0=A.accS[:, a1, 0:128], scalar1=A.sm[:, 8 + qs:9 + qs],
                                                         scalar2=None, op0=ALU.mult),
                  reads=allacc + ["rcl"], writes=["t1"])
            P.dve(lambda e, a0=a0, qs=qs: e.scalar_tensor_tensor(out=A.oS[:, qs, :], in0=A.accS[:, a0, 0:128],
                                                                scalar=A.sm[:, a0:a0 + 1], in1=A.t1[:], op0=ALU.mult, op1=ALU.add),
                  reads=allacc + ["rc8", "t1"], writes=["oS"])
        finalize_a(P, A, A.oS, "oS", 24)

    def emit_final_b(Tq):
        finalize_b(P, C, A, A.oS, "oS", 24, A.ob4, "ob4", gate=lambda j, Tq=Tq: A.G2[:, Tq * 4 + j, 0, :],
                   gate_keys=[("G2", Tq * 4 + j) for j in range(4)], split=True)

    def emit_final_c(Tq):
        q0 = Tq * 512
        finalize_b2(P, C, A.ob4, "ob4")
        flush_oT(P, C, A, oT_dst(1, hh, slice(q0, q0 + 512)))

    emit_qk(0)
    pending = []
    for s in range(len(steps)):
        if s + 1 < len(steps):
            emit_qk(s + 1)
        emit_pv(s)
        if step_hook is not None:
            step_hook(s)
        Tq, kt = steps[s]
        while pending and pending[0][0] <= s:
            _, fnp, tq = pending.pop(0)
            fnp(tq)
        if kt >= 4 * Tq + 1:
            emit_acc_copy(kt - 4 * Tq - 1)
        if kt == 4 * Tq + 3:
            emit_final(Tq)
            pending.append((s + 6, emit_final_b, Tq))
            pending.append((s + 8, emit_final_c, Tq))
    tail = [(lambda fnp=fnp, tq=tq: fnp(tq)) for _, fnp, tq in pending]
    if defer_tail:
        return tail
    for th in tail:
        th()
    return []


def finalize_a(P, A, oS, oSkey, c0):
    for j in range(4):
        P.dve(lambda e, j=j: e.scalar_tensor_tensor(out=A.sqj[:], in0=oS[:, j, :], scalar=1.0, in1=oS[:, j, :],
                                                    op0=ALU.mult, op1=ALU.mult, accum_out=A.sm2[:, c0 + j:c0 + j + 1]),
              reads=[oSkey], writes=[("ss4", c0, j)])


def finalize_b(P, C, A, oS, oSkey, c0, ob4, ob4key, gate, gate_keys, split=False):
    P.act(lambda e: e.activation(out=A.sm2[:, c0 + 4:c0 + 8], in_=A.sm2[:, c0:c0 + 4], func=AF.Ln, scale=1.0 / 128, bias=EPS),
          reads=[("ss4", c0, j) for j in range(4)], writes=[("ln4", c0)])
    P.act(lambda e: e.activation(out=A.sm2[:, c0 + 8:c0 + 12], in_=A.sm2[:, c0 + 4:c0 + 8], func=AF.Exp, scale=-0.5),
          reads=[("ln4", c0)], writes=[("rr4", c0)])
    for j in range(4):
        P.dve(lambda e, j=j: e.scalar_tensor_tensor(out=ob4[:, j, :], in0=oS[:, j, :], scalar=A.sm2[:, c0 + 8 + j:c0 + 9 + j],
                                                    in1=gate(j), op0=ALU.mult, op1=ALU.mult),
              reads=[oSkey, ("rr4", c0), gate_keys[j]], writes=[ob4key])
    if not split:
        finalize_b2(P, C, ob4, ob4key)


def finalize_b2(P, C, ob4, ob4key):
    for j in range(4):
        P.pe(lambda e, j=j: e.transpose(out=C.psT[:, j * 128:(j + 1) * 128], in_=ob4[:, j, :], identity=C.ident[:]),
             reads=[ob4key, "ident"], writes=["psT"])


def flush_oT(P, C, A, dst_dram, eng="dve"):
    sb = A.n_oTst % 2
    A.n_oTst += 1
    if eng == "dve":
        P.dve(lambda e: e.tensor_copy(out=A.oTst[sb][:], in_=C.psT[:, 0:512]),
              reads=["psT"], writes=[("oTst", sb)])
    else:
        P.act(lambda e: e.activation(out=A.oTst[sb][:], in_=C.psT[:, 0:512], func=AF.Copy),
              reads=["psT"], writes=[("oTst", sb)])
    k = ("oT_out", A.n_oTst)
    A.out_keys.append(k)
    P.dma("pool", lambda e: e.dma_start(out=dst_dram, in_=A.oTst[sb][:]), reads=[("oTst", sb)], writes=[k])


def host_weights_B(inp, l):
    w = {}
    w["wr"] = np.ascontiguousarray(inp["ret_w_o"][l])
    w["wd"] = np.ascontiguousarray(inp["diff_w_o"][l])
    w["wmr"] = np.ascontiguousarray(inp["w_in"][l][:, 7168:8192])
    w["wmd"] = np.ascontiguousarray(inp["w_in"][l][:, 8192:9216])
    w["wo"] = np.ascontiguousarray(inp["w_out"][l])
    spb = np.empty((128, 24), np.float32)
    spb[:, 0:8] = inp["norm_g"][l].reshape(8, 128).T
    spb[:, 8:16] = inp["ret_norm_g"][l].reshape(8, 128).T
    spb[:, 16:24] = inp["diff_sub_norm_g"][l].reshape(8, 128).T
    w["spb"] = spb
    return w


def alloc_B(nc, hT_own=None, al=None):
    al = al or nc.alloc_sbuf_tensor
    B = Ctx()
    B.W = {nm: al(f"B_{nm}", [128, 8, D], BF16) for nm in ("wo", "wr", "wd", "wmr", "wmd")}
    B.stg = [al(f"B_stg{i}", [128, 512], F32) for i in range(4)]
    B.spb = al("B_spb", [128, 24], F32)
    B.hT = hT_own if hT_own is not None else al("B_hT", [128, 8, 2048], BF16)
    B.oT = [al(f"B_oT{i}", [128, 16, 512], BF16) for i in range(2)]
    B.mT = al("B_mT", [128, 8, 512], BF16)
    B.sg = [al(f"B_sg{i}", [128, 512], F32) for i in range(2)]
    B.m1 = al("B_m1", [128, 512], F32)
    B.m2 = al("B_m2", [128, 512], F32)
    B.n_stg = 0
    B.out_keys = []
    return B


def stage_B_weight_thunks(P, B, l, wB, engines=("act", "dve", "act", "pool", "act", "dve")):
    lam_init = lam_init_of(l)
    out = []

    def head():
        P.dma("sp", lambda e: e.dma_start(out=B.spb[:], in_=wB["spb"]), writes=["spb"])
        P.dve(lambda e: e.tensor_scalar(out=B.spb[:, 16:24], in0=B.spb[:, 16:24], scalar1=1.0 - lam_init, scalar2=None, op0=ALU.mult),
              reads=["spb"], writes=["spb"])
    out.append(head)
    order = [(nm, g0, half) for half in range(2) for nm, g0 in (("wmr", 0), ("wmd", 0), ("wr", 8), ("wd", 16))]
    order += [("wo", None, 0), ("wo", None, 1)]
    pieces = [(nm, g0, half, c) for nm, g0, half in order for c in range(8)]

    def piece(i):
        nm, gcol0, half, c = pieces[i]
        hs = slice(half * 512, (half + 1) * 512)
        sb = i % 4
        eng = engines[i % len(engines)]
        P.dma("sp", lambda e: e.dma_start(out=B.stg[sb][:], in_=wB[nm][c * 128:(c + 1) * 128, hs]), writes=[("Bstg", sb)])
        dst = B.W[nm][:, c, hs]
        if gcol0 is None:
            if eng == "act":
                fn = lambda e: e.activation(out=dst, in_=B.stg[sb][:], func=AF.Copy)
            else:
                fn = lambda e: e.tensor_copy(out=dst, in_=B.stg[sb][:])
            P.add(eng, fn, reads=[("Bstg", sb)], writes=[("BW", nm, c, half)])
        else:
            col = B.spb[:, gcol0 + c:gcol0 + c + 1]
            if eng == "act":
                fn = lambda e: e.activation(out=dst, in_=B.stg[sb][:], func=AF.Copy, scale=col)
            elif eng == "dve":
                fn = lambda e: e.tensor_scalar(out=dst, in0=B.stg[sb][:], scalar1=col, scalar2=None, op0=ALU.mult)
            else:
                fn = lambda e: e.tensor_scalar(out=dst, in0=B.stg[sb][:], scalar1=col, scalar2=1.0, op0=ALU.mult, op1=ALU.mult)
            P.add(eng, fn, reads=[("Bstg", sb), "spb"], writes=[("BW", nm, c, half)])

    for i in range(len(pieces)):
        out.append(lambda i=i: piece(i))
    return out


def stage_B(P, nc, C, B, N, l, wB, oTB_dram, hT_dram, x_dram, xout_dram, hTn_dram, last, oT_load=None, tile_done=None, after_last_jloop=None):
    ps = C.ps
    lam_init = lam_init_of(l)
    if not getattr(B, "weights_done", False):
        for th in stage_B_weight_thunks(P, B, l, wB):
            th()
    B.weights_done = False
    def load_hT_own(T):
        P.dma("sp", lambda e, T=T: e.dma_start(out=B.hT[:, :, T * 512:(T + 1) * 512], in_=hT_dram(e, T)), writes=[("BhT", T)])

    if callable(hT_dram):
        if oT_load is None:
            for T in range(4):
                load_hT_own(T)
    elif hT_dram is not None:
        for c in range(8):
            P.dma("sp", lambda e, c=c: e.dma_start(out=B.hT[:, c, :], in_=hT_dram[c * 128:(c + 1) * 128, :]),
                  writes=[("BhT", T) for T in range(4)])
    wk = lambda nm, half=None: [("BW", nm, c, h) for c in range(8) for h in ((0, 1) if half is None else (half,))]
    carry = []
    for T in range(4):
        tok = slice(T * 512, (T + 1) * 512)
        ob = T % 2
        if oT_load is not None:
            if T == 0:
                oT_load(0, 0)
                load_hT_own(0)
                for TT in range(1, 4):
                    load_hT_own(TT)
            if T + 1 < 4:
                oT_load(T + 1, (T + 1) % 2)
        else:
          for br in range(2):
            P.dma("sp", lambda e, br=br, ob=ob, tok=tok: e.dma_start(
                out=B.oT[ob][:, br * 8:(br + 1) * 8, :], in_=oTB_dram[br, :, :, tok].rearrange("h p t -> p h t")),
                writes=[("BoT", ob, br)])
        for j in range(8):
            cols = slice(j * 128, (j + 1) * 128)
            for bi, (nm, src) in enumerate((("wr", 0), ("wd", 1))):
                for hd in range(8):
                    P.pe(lambda e, bi=bi, nm=nm, src=src, hd=hd, cols=cols, ob=ob: e.matmul(
                        ps[bi][:], lhsT=B.W[nm][:, hd, cols], rhs=B.oT[ob][:, src * 8 + hd, :], start=(hd == 0), stop=(hd == 7)),
                        reads=wk(nm, j // 4) + [("BoT", ob, src)], writes=[("ps", bi)])
            for bi, nm in ((2, "wmr"), (3, "wmd")):
                for c in range(8):
                    P.pe(lambda e, bi=bi, nm=nm, c=c, cols=cols, tok=tok: e.matmul(
                        ps[bi][:], lhsT=B.W[nm][:, c, cols], rhs=B.hT[:, c, tok], start=(c == 0), stop=(c == 7)),
                        reads=wk(nm, j // 4) + [("BhT", T)], writes=[("ps", bi)])
            if j == 1 and carry:
                for fnc in carry:
                    fnc()
                carry[:] = []
            P.act(lambda e: e.activation(out=B.sg[0][:], in_=ps[2][:], func=AF.Sigmoid), reads=[("ps", 2)], writes=[("Bsg", 0)])
            P.act(lambda e: e.activation(out=B.sg[1][:], in_=ps[3][:], func=AF.Sigmoid), reads=[("ps", 3)], writes=[("Bsg", 1)])
            P.dve(lambda e: e.tensor_tensor(out=B.m1[:], in0=ps[0][:], in1=B.sg[0][:], op=ALU.mult),
                  reads=[("ps", 0), ("Bsg", 0)], writes=["Bm1"])
            P.dve(lambda e: e.tensor_tensor(out=B.m2[:], in0=ps[1][:], in1=B.sg[1][:], op=ALU.mult),
                  reads=[("ps", 1), ("Bsg", 1)], writes=["Bm2"])
            P.pool(lambda e, j=j: e.tensor_tensor(out=B.mT[:, j, :], in0=B.m1[:], in1=B.m2[:], op=ALU.add),
                   reads=["Bm1", "Bm2"], writes=[("BmT", j)])
        pre_thunks = []
        if T == 3 and after_last_jloop is not None:
            pre_thunks = after_last_jloop() or []
        pend = []
        for tsub in range(4):
            t = T * 4 + tsub
            xb = t % 2
            banks = (4, 5) if tsub % 2 == 0 else (6, 3)
            P.dma("sp", lambda e, t=t, xb=xb: e.dma_start(out=N.xt[xb][:], in_=(x_dram(e, t) if callable(x_dram) else x_dram[t * 128:(t + 1) * 128, :])),
                  writes=[("N", "xt", xb)])
            for _ in range(3):
                if pre_thunks:
                    pre_thunks.pop(0)()
            for chh in range(2):
                bo = banks[chh]
                for j in range(8):
                    P.pe(lambda e, bo=bo, j=j, tsub=tsub, chh=chh: e.matmul(
                        ps[bo][:], lhsT=B.mT[:, j, tsub * 128:(tsub + 1) * 128], rhs=B.W["wo"][:, j, chh * 512:(chh + 1) * 512],
                        start=(j == 0), stop=(j == 7)),
                        reads=wk("wo", chh) + [("BmT", jj) for jj in range(8)], writes=[("ps", bo)])
            for fnp in pend:
                fnp()
            pend = []
            for chh in range(2):
                bo = banks[chh]
                P.dve(lambda e, bo=bo, xb=xb, chh=chh: e.tensor_tensor(out=N.xt[xb][:, chh * 512:(chh + 1) * 512], in0=ps[bo][:],
                                                                      in1=N.xt[xb][:, chh * 512:(chh + 1) * 512], op=ALU.add),
                      reads=[("ps", bo), ("N", "xt", xb)], writes=[("N", "xt", xb)])
            k = ("xout", l, t)
            B.out_keys.append(k)
            P.dma("pool", lambda e, t=t, xb=xb: e.dma_start(out=xout_dram[t * 128:(t + 1) * 128, :], in_=N.xt[xb][:]),
                  reads=[("N", "xt", xb)], writes=[k])
            if not last:
                norm_tile_a(P, C, N, xb, ("N", "xt", xb))
                pend.append(lambda xb=xb, t=t, T=T: norm_tile_t(P, C, N, xb, B.hT[:, :, t * 128:(t + 1) * 128], ("BhT", T)))
        while pre_thunks:
            pre_thunks.pop(0)()

        def tile_tail(pend=pend, T=T, tok=tok):
            for fnp in pend:
                fnp()
            if not last and callable(hTn_dram):
                k = ("hTn", l, T)
                B.out_keys.append(k)
                P.dma("pool", lambda e: e.dma_start(out=hTn_dram(T), in_=B.hT[:, :, tok]), reads=[("BhT", T)], writes=[k])
                if tile_done is not None:
                    tile_done(T, [k])

        if callable(hTn_dram) or last:
            if T < 3:
                carry.append(tile_tail)
            else:
                tile_tail()
            continue
        for fnp in pend:
            fnp()
        if not last:
            if callable(hTn_dram):
                pass
            else:
                k = ("hTn", l, T)
                B.out_keys.append(k)
                P.dma("pool", lambda e, tok=tok: e.dma_start(out=hTn_dram.rearrange("(c p) t -> p c t", p=128)[:, :, tok],
                                                            in_=B.hT[:, :, tok]),
                      reads=[("BhT", T)], writes=[k])


from concourse.bass_utils import run_bass_kernel_spmd

NCORES = 8
PAIRS = [[0, 1], [2, 3], [4, 5], [6, 7]]


def _dram_in(nc, name, arr):
    dt = {np.dtype(np.float32): F32, np.dtype(bf16): BF16}[arr.dtype]
    return nc.dram_tensor(name, list(arr.shape), dt, kind="ExternalInput").ap()


class Arena:
    def __init__(self, nc, nbytes):
        self.nbytes = nbytes
        self.t = nc.alloc_sbuf_tensor("arena", [128, nbytes // 2], BF16)
        self.off = 0
        self.peak = 0

    def reset(self):
        self.off = 0

    def alloc(self, name, shape, dtype):
        n = 1
        for d in shape[1:]:
            n *= d
        esz = 4 if dtype == F32 else 2
        size = (n * esz + 31) // 32 * 32
        assert self.off + size <= self.nbytes, (name, self.off, size, self.nbytes)
        ap = self.t[0:shape[0], self.off // 2:(self.off + n * esz) // 2]
        if dtype == F32:
            ap = ap.bitcast(F32)
        if len(shape) > 2:
            names = " ".join(f"d{i}" for i in range(len(shape) - 1))
            kw = {f"d{i}": shape[i + 1] for i in range(len(shape) - 2)}
            ap = ap.rearrange(f"p ({names}) -> p {names}", **kw)
        self.off += size
        self.peak = max(self.peak, self.off)
        return ap


def build_fused(sample, stop_stage=99, nheads=NH):
    nc = bass.Bass("TRN2", target_bir_lowering=False)
    cd = {k: _dram_in(nc, "i_" + k, v) for k, v in sample.items()}
    out = nc.dram_tensor("out", [2048, D], F32, kind="ExternalOutput").ap()
    hT_in = [[nc.dram_tensor(f"hT_in{l}_{T}", [128, 4096], BF16, kind="Internal").ap() for T in range(4)] for l in range(2)]
    hT_ag = [[nc.dram_tensor(f"hT_ag{l}_{T}", [256, 4096], BF16, kind="Internal").ap() for T in range(4)] for l in range(2)]
    oT_loc = [[nc.dram_tensor(f"oT_loc{l}_{h}", [256, S], BF16, kind="Internal").ap() for h in range(NH)] for l in range(2)]
    oT_ag = [[nc.dram_tensor(f"oT_ag{l}_{h}", [512, S], BF16, kind="Internal").ap() for h in range(NH)] for l in range(2)]
    x_mid = nc.dram_tensor("x_mid", [2048, D], F32, kind="Internal").ap()
    hT_all0 = nc.dram_tensor("hT_all0", [8 * 128, 4096], BF16, kind="Internal").ap()

    P = Prog(nc, n_dma_sems=12)
    dyn = {}
    C = alloc_common(P, nc, cd)
    AR = Arena(nc, 196 * 1024)
    A = alloc_A(nc, al=AR.alloc)
    peakA = AR.off
    dead_bytes = 65536 + 8 * WA_COLS * 2 + 2 * WA_COLS * 4 + 1088 + 2 * 1024 + 2 * 1024
    AR.reset()
    B = alloc_B(nc, al=AR.alloc)
    N = alloc_N(nc, al=AR.alloc)
    peakB = AR.off
    assert 5 * 16384 + 4 * 2048 + 96 <= dead_bytes, dead_bytes

    def ag_hT(l, T, keys):
        P.coll(lambda e: e.collective_compute("AllGather", ALU.bypass, replica_groups=PAIRS, ins=[hT_in[l][T]], outs=[hT_ag[l][T]]),
               reads=keys, writes=[("hT_ag", l, T)])

    def hTn_dst(l):
        return lambda T: hT_in[l][T].rearrange("p (c t) -> p c t", c=8)

    def hT_own(l):
        if l == 0:
            def own0(e, T):
                if "hoffs" not in dyn:
                    dyn["hoffs"] = [rpar(e) * 512 + TT * 128 for TT in range(4)]
                return hT_all0[bass.ds(dyn["hoffs"][T], 128), :]
            return own0
        return lambda e, T: hT_in[l][T].rearrange("p (c t) -> p c t", c=8)

    def hT_full(l):
        return lambda r, T: hT_ag[l][T][r * 128:(r + 1) * 128, :].rearrange("p (c t) -> p c t", c=8)

    def rpar(e):
        if "r" not in dyn:
            dyn["r"] = e.partition_id() % 2
        return dyn["r"]

    def roff(e, mult):
        if "r" not in dyn:
            dyn["r"] = e.partition_id() % 2
        if mult not in dyn:
            dyn[mult] = e.snap(e.to_reg(dyn["r"] * mult))
        return dyn[mult]

    def dyn_view(e, name, make):
        if name not in dyn:
            dyn[name] = make()
        return dyn[name]

    N0 = Ctx()
    xt4 = B.oT[0][:].rearrange("p h t -> p (h t)").bitcast(F32)
    xn4 = B.oT[1][:].rearrange("p h t -> p (h t)")
    N0.xt = [xt4[:, i * 1024:(i + 1) * 1024] for i in range(4)]
    N0.xn = [xn4[:, i * 1024:(i + 1) * 1024] for i in range(4)]
    N0.sq = N.sq
    N0.ss = [B.m1[:, i:i + 1] for i in range(4)]
    N0.rs = [B.m1[:, 8 + i:9 + i] for i in range(4)]
    def n_store(Tg):
        P.dma("pool", lambda e, Tg=Tg: e.dma_start(out=hT_all0[Tg * 128:(Tg + 1) * 128, :].rearrange("p (c t) -> p c t", c=8),
                                                   in_=A.hT[:, :, Tg * 512:(Tg + 1) * 512]),
              reads=[("hT", Tg)], writes=[("hT_all0", Tg)])

    for t in range(32 + 2):
        if t < 32:
            b = t % 4
            P.dma("sp" if t % 2 == 0 else "pool", lambda e, t=t, b=b: e.dma_start(out=N0.xt[b], in_=cd["x"][t * 128:(t + 1) * 128, :]),
                  writes=[("N", "xt", b)])
            norm_tile_a(P, C, N0, b, ("N", "xt", b), sq_eng=("dve" if t % 2 else "act"), stats_only=True)
        if 0 <= t - 1 < 32:
            norm_tile_scale(P, N0, (t - 1) % 4, ("N", "xt", (t - 1) % 4))
        if 0 <= t - 2 < 32:
            u = t - 2
            norm_tile_t(P, C, N0, u % 4, A.hT[:, :, u * 128:(u + 1) * 128], ("hT", u // 4), eng=("act" if u % 2 else "dve"))
            if u % 4 == 3:
                n_store(u // 4)
    stage_A_pre(P, A, cd["wA0"], cd["sp0"])
    P.fence()

    def finish():
        P.fence()
        stats = P.emit()
        return nc, stats, (peakA, peakB)

    A.nheads = nheads
    stage_no = 1
    for l in range(2):
        if stop_stage <= stage_no:
            return finish()
        stage_no += 2
        last = l == 1
        hT_src = hT_full(l) if l > 0 else None

        def oT_dst(br, hh, tsl, l=l):
            return oT_loc[l][hh][br * 128:(br + 1) * 128, tsl]

        nk0 = [0]

        def head_done(hh, l=l):
            keys = A.out_keys[nk0[0]:]
            nk0[0] = len(A.out_keys)
            P.coll(lambda e: e.collective_compute("AllGather", ALU.bypass, replica_groups=PAIRS,
                                                  ins=[oT_loc[l][hh]], outs=[oT_ag[l][hh]]),
                   reads=keys, writes=[("oT_ag", l, hh)])

        nk0[0] = len(A.out_keys)
        wB = {nm: cd[f"{nm}{l}"] for nm in ("wr", "wd", "wmr", "wmd", "wo", "spb")}
        thunks = stage_B_weight_thunks(P, B, l, wB, engines=("pool",))
        tpos = [0]

        def last_proj_done():
            P.barrier_on(A.dead_keys, ("sp", "pool", "dve"))
            thunks[0]()
            tpos[0] = 1

        def step_hook(s):
            n = 1 if s % 2 == 0 else 0
            if s >= 20:
                n = 1
            for _ in range(n):
                if tpos[0] < len(thunks):
                    thunks[tpos[0]]()
                    tpos[0] += 1

        stage_A(P, nc, C, A, l, None, hT_src, cd[f"wA{l}"], cd[f"sp{l}"], cd["augk"], cd["augq"], None, True,
                oT_dst=oT_dst, head_done=head_done, last_proj_done=last_proj_done, step_hook=step_hook, pre_done=True)
        while tpos[0] < len(thunks):
            thunks[tpos[0]]()
            tpos[0] += 1
        B.weights_done = True
        P.fence()
        if stop_stage <= stage_no - 1:
            return finish()

        def oT_load(T, ob, l=l):
            for hh in range(NH):
                for s in range(2):
                    for br in range(2):
                        def fn(e, hh=hh, s=s, br=br, T=T, ob=ob):
                            if "offs" not in dyn:
                                dyn["offs"] = [rpar(e) * 2048 + TT * 512 for TT in range(4)]
                            src = oT_ag[l][hh][s * 256 + br * 128:s * 256 + (br + 1) * 128, bass.ds(dyn["offs"][T], 512)]
                            return e.dma_start(out=B.oT[ob][:, br * 8 + 4 * s + hh, :], in_=src)
                        P.dma("sp", fn, reads=[("oT_ag", l, hh)], writes=[("BoT", ob, br)])

        def after_last_jloop(l=l):
            P.barrier_on([("BW", "wmd", c, h) for c in range(8) for h in range(2)] + [("Bstg", i) for i in range(4)], ("sp", "pool"))
            return stage_A_pre_thunks(P, A, cd[f"wA{l + 1}"], cd[f"sp{l + 1}"])

        x_src = cd["x_own"] if l == 0 else x_mid
        x_dst = x_mid if l == 0 else out
        nb0 = len(B.out_keys)
        stage_B(P, nc, C, B, N, l, wB, None, hT_own(l), x_src, x_dst, None if last else hTn_dst(l + 1), last, oT_load=oT_load,
                tile_done=(None if last else (lambda T, keys, l=l: ag_hT(l + 1, T, keys))),
                after_last_jloop=(None if last else after_last_jloop))
        P.fence()
    stats = P.emit()
    return nc, stats, (peakA, peakB)


def host_inputs(inp):
    consts = [host_consts(g) for g in range(2)]
    x = inp["x"].astype(np.float32, copy=False)
    wbs = [host_weights_B(inp, l) for l in range(2)]
    was = [[host_weights_A(inp, l, g) for g in range(2)] for l in range(2)]
    maps = []
    for c in range(NCORES):
        b, g = c // 2, c % 2
        m = dict(consts[g])
        m["x"] = x[b]
        m["x_own"] = np.ascontiguousarray(x[b, g * 2048:(g + 1) * 2048])
        for l in range(2):
            m[f"wA{l}"], m[f"sp{l}"] = was[l][g]
            for nm, v in wbs[l].items():
                m[f"{nm}{l}"] = v
        maps.append(m)
    return maps


def kernel(**inputs):
    inp = {k: np.asarray(v) for k, v in inputs.items()}
    maps = host_inputs(inp)
    nc, stats, peaks = build_fused(maps[0])
    res = run_bass_kernel_spmd(nc, [{"i_" + k: v for k, v in m.items()} for m in maps], core_ids=list(range(NCORES)))
    out = np.empty((4, S, D), np.float32)
    for c in range(NCORES):
        out[c // 2, (c % 2) * 2048:(c % 2 + 1) * 2048] = res.results[c]["out"]
    return out
```

```python
import numpy as np, math
import concourse.bass as bass
import concourse.mybir as mybir

F32 = mybir.dt.float32
BF16 = mybir.dt.bfloat16
AF = mybir.ActivationFunctionType
ALU = mybir.AluOpType
AX = mybir.AxisListType


class Op:
    __slots__ = ("idx", "eng", "fn", "deps", "dma", "tok", "sig", "qprev", "cc")

    def __init__(self, idx, eng, fn, deps, dma, cc=False):
        self.idx = idx; self.eng = eng; self.fn = fn; self.deps = deps
        self.dma = dma; self.tok = None; self.sig = False; self.qprev = None; self.cc = cc


class Prog:
    ENGS = ("pe", "act", "dve", "pool", "sp")

    def __init__(self, nc, n_dma_sems=6):
        self.nc = nc
        self.ops = []
        self.lastw = {}
        self.readers = {}
        self.K = n_dma_sems
        self.last_eng = {}
        self.async_since_fence = []

    @staticmethod
    def _excl(k):
        return k == "psT" or (isinstance(k, tuple) and k[0] == "ps")

    def add(self, eng, fn, reads=(), writes=(), dma=False, cc=False, extra_deps=()):
        writes = list(writes) + [k for k in reads if self._excl(k)]
        reads = [k for k in reads if not self._excl(k)]
        deps = set()
        for k in reads:
            w = self.lastw.get(k)
            if w is not None:
                deps.add(w)
        for k in writes:
            w = self.lastw.get(k)
            if w is not None:
                deps.add(w)
            for r in self.readers.get(k, ()):
                deps.add(r)
        idx = len(self.ops)
        deps.update(extra_deps)
        deps.discard(idx)
        op = Op(idx, eng, fn, deps, dma or cc, cc)
        self.ops.append(op)
        if fn is not None:
            if dma or cc:
                self.async_since_fence.append(idx)
            else:
                self.last_eng[eng] = idx
        for k in writes:
            self.lastw[k] = idx
            self.readers[k] = []
        for k in reads:
            if k in writes:
                continue
            self.readers.setdefault(k, []).append(idx)
        return idx

    def pe(self, fn, reads=(), writes=()): return self.add("pe", fn, reads, writes)
    def act(self, fn, reads=(), writes=()): return self.add("act", fn, reads, writes)
    def dve(self, fn, reads=(), writes=()): return self.add("dve", fn, reads, writes)
    def pool(self, fn, reads=(), writes=()): return self.add("pool", fn, reads, writes)
    def dma(self, q, fn, reads=(), writes=()): return self.add(q, fn, reads, writes, dma=True)

    def coll(self, fn, reads=(), writes=()):
        return self.add("pool", fn, reads, writes, cc=True)

    def barrier_on(self, keys, engines):
        deps = set()
        for k in keys:
            w = self.lastw.get(k)
            if w is not None:
                deps.add(w)
            deps.update(self.readers.get(k, ()))
        for e in engines:
            self.add(e, None, extra_deps=deps)

    def fence(self):
        deps = set(self.last_eng.values()) | set(self.async_since_fence)
        for e in self.ENGS:
            self.add(e, None, extra_deps=deps)
        self.async_since_fence = []

    def emit(self):
        nc = self.nc
        ops = self.ops
        for op in ops:
            for d in op.deps:
                dop = ops[d]
                if dop.dma:
                    continue
                if dop.eng == "pe" and op.eng == "pe" and not op.dma:
                    continue
                dop.sig = True
        eng_sem = {}
        cnt = {}
        dma_sems = {}
        dma_cnt = {}
        dma_hist = {}
        for op in ops:
            if op.fn is None:
                continue
            if op.cc:
                op.tok = (nc.alloc_semaphore(f"cc_{op.idx}"), 1)
            elif op.dma:
                q = op.eng
                if q not in dma_sems:
                    dma_sems[q] = [nc.alloc_semaphore(f"dq_{q}_{i}") for i in range(self.K)]
                    dma_cnt[q] = 0
                    dma_hist[q] = []
                i = dma_cnt[q]
                dma_cnt[q] += 1
                op.tok = (dma_sems[q][i % self.K], 16 * (i // self.K + 1))
                if i >= self.K:
                    op.qprev = dma_hist[q][i - self.K]
                dma_hist[q].append(op.idx)
            elif op.sig:
                e = op.eng
                if e not in eng_sem:
                    eng_sem[e] = nc.alloc_semaphore(f"es_{e}")
                    cnt[e] = 0
                cnt[e] += 1
                op.tok = (eng_sem[e], cnt[e])
        by_eng = {e: [op for op in ops if op.eng == e] for e in self.ENGS}
        self.stats = {e: len(v) for e, v in by_eng.items()}
        nwaits = [0]

        def run_engine(e, lst):
            waited = {}
            for op in lst:
                need = []
                for d in sorted(op.deps):
                    dop = ops[d]
                    if dop.fn is None:
                        continue
                    if (not dop.dma) and dop.eng == "pe" and op.eng == "pe" and not op.dma:
                        continue
                    need.append(dop.tok)
                if op.qprev is not None:
                    need.append(ops[op.qprev].tok)
                best = {}
                for sem, val in need:
                    k = id(sem)
                    if k not in best or best[k][1] < val:
                        best[k] = (sem, val)
                for k, (sem, val) in best.items():
                    if waited.get(k, 0) >= val:
                        continue
                    e.wait_ge(sem, val)
                    nwaits[0] += 1
                    waited[k] = val
                if op.fn is None:
                    continue
                ins = op.fn(e)
                if op.tok is not None:
                    if op.cc:
                        ins.then_inc(op.tok[0])
                    else:
                        ins.then_inc(op.tok[0], 16 if op.dma else 1)

        with nc.Block() as block:
            for ename, battr in (("sp", "sync"), ("pe", "tensor"), ("act", "scalar"),
                                 ("dve", "vector"), ("pool", "gpsimd")):
                lst = by_eng[ename]
                if not lst:
                    continue
                getattr(block, battr)(lambda e, lst=lst: run_engine(e, lst))
        self.stats["waits"] = nwaits[0]
        return self.stats


import math
import numpy as np
import ml_dtypes

D = 1024
S = 4096
NT = 32
EPS = 1e-6
NH = 4
WA_COLS = 896
ACCW = 160
bf16 = ml_dtypes.bfloat16


def lam_init_of(l):
    return 0.8 - 0.6 * math.exp(-0.3 * l)


def host_consts(g):
    c = {}
    c["ident"] = np.eye(128, dtype=np.float32).astype(bf16)
    blk = np.zeros((128, 128), np.float32)
    blk[:64, :64] = 1.0 / 64
    blk[64:, 64:] = 1.0 / 64
    c["blk64"] = blk.astype(bf16)
    kk = np.arange(128)[:, None]
    qq = np.arange(128)[None, :]
    c["cmask"] = np.where(qq >= kk, 0.0, -30000.0).astype(np.float32).astype(bf16)
    t = np.arange(S)
    augk = np.stack([t % 128, t // 128, np.ones(S), np.ones(S)]).astype(np.float32)
    c["augk"] = augk.astype(bf16)
    augq = np.zeros((NH, 4, S), np.float32)
    dmask = np.zeros((128, NH, 128), np.float32)
    qdec = np.zeros((64, NH, 512), np.float32)
    kdec = np.zeros((128, 2 * NH), np.float32)
    for hh in range(NH):
        H = 4 * g + hh
        slope = 2.0 ** (-(H + 1))
        augq[hh, 0] = slope
        augq[hh, 1] = 128 * slope
        augq[hh, 2] = -slope * (t % 128)
        augq[hh, 3] = -128 * slope * (t // 128)
        lg = math.log1p(-2.0 ** (-5 - H))
        rel = qq - kk
        dmask[:, hh, :] = np.where(rel >= 0, np.exp(np.maximum(rel, 0) * lg), 0.0)
        qdec[:, hh, :] = np.exp(((np.arange(512) % 128) + 1.0) * lg)[None, :]
        kdec[:, hh] = np.exp((127.0 - np.arange(128)) * lg) * 0.125
        kdec[:, NH + hh] = chunk_decay(H)
    assert np.array_equal(augq.astype(bf16).astype(np.float32), augq)
    assert np.array_equal(augk.astype(bf16).astype(np.float32), augk)
    c["augq"] = augq.astype(bf16)
    c["dmask"] = dmask
    c["qdec"] = qdec
    c["kdec"] = kdec
    return c


def chunk_decay(H):
    return math.exp(128.0 * math.log1p(-2.0 ** (-5 - H)))


def host_weights_A(inp, l, g):
    w = inp["w_in"][l]
    out = np.empty((NH, D, WA_COLS), np.float32)
    for hh in range(NH):
        H = 4 * g + hh
        out[hh, :, 0:128] = w[:, 3072 + H * 128: 3072 + (H + 1) * 128]
        out[hh, :, 128:256] = w[:, 4096 + H * 128: 4096 + (H + 1) * 128]
        out[hh, :, 256:320] = w[:, 0 + H * 64: (H + 1) * 64]
        out[hh, :, 320:384] = w[:, 512 + H * 64: 512 + (H + 1) * 64]
        out[hh, :, 384:512] = w[:, 5120 + H * 128: 5120 + (H + 1) * 128]
        out[hh, :, 512:640] = w[:, 6144 + H * 128: 6144 + (H + 1) * 128]
        out[hh, :, 640:768] = w[:, 1024 + H * 128: 1024 + (H + 1) * 128]
        out[hh, :, 768:896] = w[:, 2048 + H * 128: 2048 + (H + 1) * 128]
    wA = np.ascontiguousarray(out.reshape(NH, 8, 128, WA_COLS))
    sp = np.empty((128, 8 + 2 + 256), np.float32)
    sp[:, 0:8] = inp["norm_g"][l].reshape(8, 128).T
    sp[:, 8] = np.tile(inp["diff_q_norm_g"][l], 2)
    sp[:, 9] = np.tile(inp["diff_k_norm_g"][l], 2)
    lv = np.concatenate([inp["diff_lq1"][l], inp["diff_lk1"][l], inp["diff_lq2"][l], inp["diff_lk2"][l]])
    sp[:, 10:] = np.broadcast_to(lv[None, :], (128, 256))
    return wA, sp


class Ctx:
    pass


def alloc_common(P, nc, cd):
    C = Ctx()
    C.ps = [nc.alloc_psum_tensor(f"ps{i}", [128, 512], F32) for i in range(7)]
    C.psT = nc.alloc_psum_tensor("psT", [128, 1024], BF16)
    C.ident = nc.alloc_sbuf_tensor("c_ident", [128, 128], BF16)
    C.blk64 = nc.alloc_sbuf_tensor("c_blk64", [128, 128], BF16)
    C.cmask = nc.alloc_sbuf_tensor("c_cmask", [128, 128], BF16)
    C.dmask = nc.alloc_sbuf_tensor("c_dmask", [128, NH, 128], F32)
    C.qdec = nc.alloc_sbuf_tensor("c_qdec", [64, NH, 512], F32)
    C.kdec = nc.alloc_sbuf_tensor("c_kdec", [128, 2 * NH], F32)
    for nm in ("ident", "blk64", "cmask", "dmask", "qdec", "kdec"):
        t = getattr(C, nm)
        P.dma("sp", lambda e, t=t, nm=nm: e.dma_start(out=t[:], in_=cd[nm]), writes=[nm])
    return C


def alloc_N(nc, tag="N", al=None):
    al = al or nc.alloc_sbuf_tensor
    N = Ctx()
    N.xt = [al(f"{tag}_xt{i}", [128, D], F32) for i in range(2)]
    N.sq = al(f"{tag}_sq", [128, D], BF16)
    N.xn = [al(f"{tag}_xn{i}", [128, D], BF16) for i in range(2)]
    N.ss = [al(f"{tag}_ss{i}", [128, 1], F32) for i in range(2)]
    N.rs = [al(f"{tag}_rs{i}", [128, 1], F32) for i in range(2)]
    return N


def norm_tile(P, C, N, b, src_key, dst, dst_key):
    norm_tile_a(P, C, N, b, src_key)
    norm_tile_t(P, C, N, b, dst, dst_key)


def norm_tile_a(P, C, N, b, src_key, sq_eng="act", stats_only=False):
    if sq_eng == "act":
        P.act(lambda e: e.activation(out=N.xn[b][:], in_=N.xt[b][:], func=AF.Square, accum_out=N.ss[b][:]),
              reads=[src_key], writes=[("N", "xn", b), ("N", "ss", b)])
    else:
        P.dve(lambda e: e.scalar_tensor_tensor(out=N.xn[b][:], in0=N.xt[b][:], scalar=1.0, in1=N.xt[b][:], op0=ALU.mult, op1=ALU.mult,
                                               accum_out=N.ss[b][:]),
              reads=[src_key], writes=[("N", "xn", b), ("N", "ss", b)])
    P.act(lambda e: e.activation(out=N.ss[b][:], in_=N.ss[b][:], func=AF.Ln, scale=1.0 / D, bias=EPS),
          reads=[("N", "ss", b)], writes=[("N", "ss", b)])
    P.act(lambda e: e.activation(out=N.rs[b][:], in_=N.ss[b][:], func=AF.Exp, scale=-0.5),
          reads=[("N", "ss", b)], writes=[("N", "rs", b)])
    if not stats_only:
        norm_tile_scale(P, N, b, src_key)


def norm_tile_scale(P, N, b, src_key):
    P.dve(lambda e: e.tensor_scalar(out=N.xn[b][:], in0=N.xt[b][:], scalar1=N.rs[b][:], scalar2=None, op0=ALU.mult),
          reads=[src_key, ("N", "rs", b)], writes=[("N", "xn", b)])


def norm_tile_t(P, C, N, b, dst, dst_key, eng="act"):
    for c in range(8):
        P.pe(lambda e, c=c: e.transpose(out=C.psT[:, c * 128:(c + 1) * 128], in_=N.xn[b][:, c * 128:(c + 1) * 128],
                                        identity=C.ident[:]),
             reads=[("N", "xn", b), "ident"], writes=["psT"])
    if eng == "act":
        P.act(lambda e: e.activation(out=dst, in_=C.psT[:].rearrange("p (c t) -> p c t", c=8), func=AF.Copy),
              reads=["psT"], writes=[dst_key])
    else:
        P.dve(lambda e: e.tensor_copy(out=dst, in_=C.psT[:].rearrange("p (c t) -> p c t", c=8)),
              reads=["psT"], writes=[dst_key])


def alloc_A(nc, al=None):
    al = al or nc.alloc_sbuf_tensor
    A = Ctx()
    A.hT = al("sb_hT", [128, 8, S], BF16)
    A.wbf = al("wbf", [128, 8, WA_COLS], BF16)
    A.wstg = [al(f"wstg{i}", [128, 1, WA_COLS], F32) for i in range(2)]
    A.sp = al("spar", [128, 8 + 2 + 256], F32)
    A.sgt = [al(f"sgt{i}", [128, 2, 128], F32) for i in range(2)]
    A.sqb2 = [al(f"sqb{i}", [128, 512], BF16) for i in range(2)]
    A.dead_keys = ([("hT", T) for T in range(8)] + [("wbf", c) for c in range(8)] + [("wstg", i) for i in range(2)]
                   + ["spar"] + [("sgt", i) for i in range(2)] + [("sqb", i) for i in range(2)])
    A.accS = al("accS", [128, 8, 129], F32)
    A.QT = [al(f"QT{j}", [128, S], BF16) for j in range(2)]
    A.KT = [al(f"KT{j}", [128, S], BF16) for j in range(2)]
    A.RQ = al("RQ", [128, S], BF16)
    A.RK = al("RK", [128, S], BF16)
    A.KrT = al("KrT", [128, NT, 64], BF16)
    A.V2 = al("V2", [128, NT, 2, 129], BF16)
    A.G2 = al("G2", [128, NT, 2, 128], BF16)
    A.PT = [[al(f"PT{j}{b}", [128, 512], BF16) for b in range(2)] for j in range(2)]
    A.lnv = al("lnv", [128, 512], F32)
    A.rstd = al("rstd", [128, 512], F32)
    A.Sf = [al(f"Sf{i}", [64, 128], F32) for i in range(2)]
    A.sm = al("sm", [128, 16], F32)
    A.sm2 = al("sm2", [128, 40], F32)
    A.lam = al("lamt", [128, 8], F32)
    A.lprod = al("lprod", [128, 2, 64], F32)
    A.t1 = al("t1", [128, 128], F32)
    A.oS = al("oS", [128, 4, 128], F32)
    A.sqj = al("sqj", [128, 128], F32)
    A.ob4 = al("ob4", [128, 4, 128], BF16)
    A.oTst = [al(f"oTst{i}", [128, 512], BF16) for i in range(2)]
    A.n_oTst = 0
    A.out_keys = []
    return A


def emit_A_weights(P, A, wA_dram, hh):
    for c in range(8):
        sb = c % 2
        P.dma("sp", lambda e, c=c, sb=sb, hh=hh: e.dma_start(out=A.wstg[sb][:, 0, :], in_=wA_dram[hh, c, :, :]),
              writes=[("wstg", sb)])
        P.pool(lambda e, c=c, sb=sb: e.tensor_scalar(out=A.wbf[:, c, :], in0=A.wstg[sb][:, 0, :],
                                                     scalar1=A.sp[:, c:c + 1], scalar2=1.0,
                                                     op0=ALU.mult, op1=ALU.mult),
               reads=[("wstg", sb), "spar"], writes=[("wbf", c)])


def stage_A_pre_thunks(P, A, wA_dram, sp_dram):
    out = [lambda: P.dma("sp", lambda e: e.dma_start(out=A.sp[:], in_=sp_dram), writes=["spar"])]

    def piece(c):
        sb = c % 2
        P.dma("sp", lambda e: e.dma_start(out=A.wstg[sb][:, 0, :], in_=wA_dram[0, c, :, :]), writes=[("wstg", sb)])
        P.pool(lambda e: e.tensor_scalar(out=A.wbf[:, c, :], in0=A.wstg[sb][:, 0, :], scalar1=A.sp[:, c:c + 1], scalar2=1.0,
                                         op0=ALU.mult, op1=ALU.mult),
               reads=[("wstg", sb), "spar"], writes=[("wbf", c)])
    for c in range(8):
        out.append(lambda c=c: piece(c))
    return out


def stage_A_pre(P, A, wA_dram, sp_dram):
    P.dma("sp", lambda e: e.dma_start(out=A.sp[:], in_=sp_dram), writes=["spar"])
    emit_A_weights(P, A, wA_dram, 0)


def stage_A(P, nc, C, A, l, g, hT_dram, wA_dram, sp_dram, augk_dram, augq_dram, oT_dram, first, oT_dst=None, head_done=None, last_proj_done=None, step_hook=None, pre_done=False):
    ps = C.ps
    lam_init = lam_init_of(l)
    if oT_dst is None:
        oT_dst = lambda br, hh, tsl: oT_dram[br, hh, :, tsl]
    if hT_dram is None:
        pass
    elif callable(hT_dram):
        for r in range(2):
            for T in range(4):
                tsl = slice(r * 2048 + T * 512, r * 2048 + (T + 1) * 512)
                P.dma("sp", lambda e, r=r, T=T, tsl=tsl: e.dma_start(out=A.hT[:, :, tsl], in_=hT_dram(r, T)),
                      writes=[("hT", r * 4 + T)])
    else:
        for r in range(2):
            for c in range(8):
                P.dma("sp", lambda e, r=r, c=c: e.dma_start(out=A.hT[:, c, r * 2048:(r + 1) * 2048],
                                                           in_=hT_dram[r, c * 128:(c + 1) * 128, :]),
                      writes=[("hT", r * 4 + i) for i in range(4)])
    if not pre_done:
        stage_A_pre(P, A, wA_dram, sp_dram)
    if first:
        P.pool(lambda e: e.memset(A.V2[:, :, :, 128:129], 1.0), writes=["V2ones"])
        for j in range(2):
            P.pool(lambda e, j=j: e.memset(A.KT[j][64:128, :], 0.0), writes=[("KTaug", j)])
            P.pool(lambda e, j=j: e.memset(A.QT[j][64:128, :], 0.0), writes=[("QTaug", j)])
            P.dma("sp", lambda e, j=j: e.dma_start(out=A.KT[j][64:68, :], in_=augk_dram), writes=[("KTaug", j)])
    P.dve(lambda e: e.tensor_scalar(out=A.sm[:, 14:15], in0=A.sp[:, 8:9], scalar1=0.125, scalar2=None, op0=ALU.mult),
          reads=["spar"], writes=["gq"])
    P.dve(lambda e: e.tensor_copy(out=A.sm[:, 15:16], in_=A.sp[:, 9:10]), reads=["spar"], writes=["gk"])
    lv = A.sp[:, 10:266].rearrange("p (a k) -> p a k", a=4)
    P.dve(lambda e: e.tensor_tensor(out=A.lprod[:], in0=lv[:, 0:4:2, :], in1=lv[:, 1:4:2, :], op=ALU.mult),
          reads=["spar"], writes=["lprod"])
    P.dve(lambda e: e.tensor_reduce(out=A.lam[:, 0:2], in_=A.lprod[:], axis=AX.X, op=ALU.add),
          reads=["lprod"], writes=["lam01"])
    P.act(lambda e: e.activation(out=A.lam[:, 2:4], in_=A.lam[:, 0:2], func=AF.Exp), reads=["lam01"], writes=["lam23"])
    P.dve(lambda e: e.tensor_tensor(out=A.lam[:, 4:5], in0=A.lam[:, 3:4], in1=A.lam[:, 2:3], op=ALU.subtract),
          reads=["lam23"], writes=["lam4"])
    P.dve(lambda e: e.tensor_scalar(out=A.lam[:, 5:6], in0=A.lam[:, 4:5], scalar1=-lam_init, scalar2=None, op0=ALU.add),
          reads=["lam4"], writes=["neglam"])
    neglam = A.lam[:, 5:6]

    hT_keys = lambda T: [("hT", T)]

    stop = getattr(A, "stop", 99)
    nheads = getattr(A, "nheads", NH)

    def emit_weights(hh):
        emit_A_weights(P, A, wA_dram, hh)

    tail_thunks = []
    tail_head = [None]
    for hh in range(nheads):
        cdec = C.kdec[0:64, NH + hh:NH + hh + 1]
        wkeys = [("wbf", c) for c in range(8)]
        for j in range(2):
            P.dma("sp", lambda e, j=j, hh=hh: e.dma_start(out=A.QT[j][64:68, :], in_=augq_dram[hh]),
                  writes=[("QTaug", j)])
        if stop <= 0:
            continue
        ring = (0, 1, 2, 4)
        statb = (3, 5)
        groups = [(T, gi) for T in range(8) for gi in range(3)]

        def f_a(n):
            T, gi = groups[n]
            b = ring[n % 4]
            tok = slice(T * 512, (T + 1) * 512)
            for c in range(8):
                P.pe(lambda e, b=b, c=c, gi=gi, tok=tok: e.matmul(ps[b][:], lhsT=A.wbf[:, c, gi * 128:(gi + 1) * 128],
                                                                 rhs=A.hT[:, c, tok], start=(c == 0), stop=(c == 7)),
                     reads=wkeys + hT_keys(T), writes=[("ps", b)])

        def f_b(n, hh=hh):
            T, gi = groups[n]
            b = ring[n % 4]
            tok = slice(T * 512, (T + 1) * 512)
            if gi < 2:
                P.act(lambda e, b=b, n=n: e.activation(out=A.sqb2[n % 2][:], in_=ps[b][:], func=AF.Square),
                      reads=[("ps", b)], writes=[("sqb", n % 2)])
            else:
                P.dve(lambda e, b=b, tok=tok: e.tensor_copy(out=A.RQ[0:64, tok], in_=ps[b][0:64, :]),
                      reads=[("ps", b)], writes=[("RQ", T)])
                P.dve(lambda e, b=b, tok=tok, hh=hh: e.tensor_tensor(out=A.RQ[64:128, tok], in0=ps[b][0:64, :],
                                                                    in1=C.qdec[:, hh, :], op=ALU.mult),
                      reads=[("ps", b), "qdec"], writes=[("RQd", T)])
                P.dve(lambda e, b=b, tok=tok: e.tensor_scalar(out=A.RK[0:64, tok], in0=ps[b][64:128, :], scalar1=0.125, scalar2=None,
                                                             op0=ALU.mult),
                      reads=[("ps", b)], writes=[("RK", T)])

        def f_cd(n):
            T, gi = groups[n]
            if gi >= 2:
                return
            b = ring[n % 4]
            sbk = statb[n % 2]
            tok = slice(T * 512, (T + 1) * 512)
            dst = A.QT if gi == 0 else A.KT
            dkey = "QT" if gi == 0 else "KT"
            gcol = A.sm[:, 14:15] if gi == 0 else A.sm[:, 15:16]
            gkey = "gq" if gi == 0 else "gk"
            P.pe(lambda e, sbk=sbk, n=n: e.matmul(ps[sbk][:], lhsT=C.blk64[:], rhs=A.sqb2[n % 2][:], start=True, stop=True),
                 reads=[("sqb", n % 2), "blk64"], writes=[("ps", sbk)])
            P.act(lambda e, sbk=sbk: e.activation(out=A.lnv[:], in_=ps[sbk][:], func=AF.Ln, bias=EPS),
                  reads=[("ps", sbk)], writes=["lnv"])
            P.act(lambda e: e.activation(out=A.rstd[:], in_=A.lnv[:], func=AF.Exp, scale=-0.5), reads=["lnv"], writes=["rstd"])
            for j in range(2):
                pr = slice(64 * j, 64 * j + 64)
                P.dve(lambda e, b=b, j=j, pr=pr, dst=dst, gcol=gcol, tok=tok: e.scalar_tensor_tensor(
                    out=dst[j][0:64, tok], in0=ps[b][pr, :], scalar=gcol[pr, :], in1=A.rstd[pr, :],
                    op0=ALU.mult, op1=ALU.mult),
                    reads=[("ps", b), "rstd", gkey], writes=[(dkey, j, T)])

        for n in range(len(groups)):
            f_a(n)
            f_b(n)
            if n >= 1:
                f_cd(n - 1)
            if n == 2 and tail_thunks:
                for th in tail_thunks:
                    th()
                tail_thunks[:] = []
                if head_done is not None:
                    head_done(tail_head[0])
        f_cd(len(groups) - 1)
        if stop <= 1:
            continue
        P.dve(lambda e: e.memset(A.Sf[0][:], 0.0), writes=[("Sf", 0)])
        P.pool(lambda e: e.memset(A.RK[64:128, 0:128], 0.0), writes=[("Sb", 0)])
        n_state_done = [0]

        def state_steps(i0, i1, cdec=cdec):
            for i in range(i0, i1):
                b = (i // 4) % 2
                sl = slice((i % 4) * 128, (i % 4 + 1) * 128)
                if i % 4 == 3:
                    P.pe(lambda e, b=b, sl=sl, i=i: e.matmul(ps[b][0:64, sl], lhsT=A.KrT[:, i, :], rhs=A.V2[:, i, 1, 0:128],
                                                            start=True, stop=True),
                         reads=[("KrT", i // 4), ("V2", i)], writes=[("ps", b)])
                    continue
                P.pe(lambda e, b=b, sl=sl, i=i: e.matmul(ps[b][:, sl], lhsT=A.KrT[:, i:i + 2, :].rearrange("p a d -> p (a d)"),
                                                        rhs=A.V2[:, i, 1, 0:128], start=True, stop=True),
                     reads=[("KrT", i // 4), ("KrT", (i + 1) // 4), ("V2", i)], writes=[("ps", b)])
            for i in range(i0, i1):
                b = (i // 4) % 2
                sl = slice((i % 4) * 128, (i % 4 + 1) * 128)
                P.dve(lambda e, b=b, sl=sl, i=i: e.scalar_tensor_tensor(out=A.Sf[(i + 1) % 2][:], in0=A.Sf[i % 2][:], scalar=cdec,
                                                                       in1=ps[b][0:64, sl], op0=ALU.mult, op1=ALU.add),
                      reads=[("ps", b), ("Sf", i % 2), "kdec"], writes=[("Sf", (i + 1) % 2)])
                P.act(lambda e, i=i: e.activation(out=A.RK[64:128, (i + 1) * 128:(i + 2) * 128], in_=A.Sf[(i + 1) % 2][:], func=AF.Copy),
                      reads=[("Sf", (i + 1) % 2)], writes=[("Sb", i + 1)])
            n_state_done[0] = i1

        def rbanks(g):
            return (2, 3) if g < 7 else (2 + g % 2, 4 + g % 2)

        oSb = [A.lnv[:].rearrange("p (a e) -> p a e", a=4), A.rstd[:].rearrange("p (a e) -> p a e", a=4)]
        oSk = ["lnv", "rstd"]
        scT4 = [A.PT[0][0][:].rearrange("p (a e) -> p a e", a=4), A.PT[0][1][:].rearrange("p (a e) -> p a e", a=4)]
        scTk = [("PT", 0, 0), ("PT", 0, 1)]
        ob4r = [A.PT[1][0][:].rearrange("p (a e) -> p a e", a=4), A.PT[1][1][:].rearrange("p (a e) -> p a e", a=4)]
        ob4k = [("PT", 1, 0), ("PT", 1, 1)]
        dm4 = C.dmask[:, hh, :].unsqueeze(1).broadcast_to([128, 4, 128])

        def r_s1(g):
            bS = rbanks(g)[0]
            for j in range(4):
                i = 4 * g + j
                ch = slice(i * 128, (i + 1) * 128)
                P.pe(lambda e, bS=bS, j=j, ch=ch: e.matmul(ps[bS][:, j * 128:(j + 1) * 128], lhsT=A.RK[0:64, ch],
                                                          rhs=A.RQ[0:64, ch], start=True, stop=True),
                     reads=[("RK", g), ("RQ", g)], writes=[("ps", bS)])

        def r_s2(g):
            bS = rbanks(g)[0]
            P.dve(lambda e, bS=bS, g=g, dm4=dm4: e.tensor_tensor(out=scT4[g % 2], in0=ps[bS][:].rearrange("p (a e) -> p a e", a=4),
                                                                 in1=dm4, op=ALU.mult),
                  reads=[("ps", bS), "dmask"], writes=[scTk[g % 2]])

        def r_s3(g):
            bO = rbanks(g)[1]
            for j in range(4):
                i = 4 * g + j
                P.pe(lambda e, bO=bO, j=j, g=g, i=i: e.matmul(ps[bO][:, j * 128:(j + 1) * 128], lhsT=scT4[g % 2][:, j, :],
                                                             rhs=A.V2[:, i, 1, 0:128], start=(j == 0), stop=False,
                                                             skip_group_check=True),
                     reads=[scTk[g % 2], ("V2", i)], writes=[("ps", bO)])
            for j in range(4):
                i = 4 * g + j
                ch = slice(i * 128, (i + 1) * 128)
                P.pe(lambda e, bO=bO, j=j, ch=ch: e.matmul(ps[bO][:, j * 128:(j + 1) * 128], lhsT=A.RQ[64:128, ch],
                                                          rhs=A.RK[64:128, ch], start=False, stop=True, skip_group_check=True),
                     reads=[("RQd", g), ("Sb", i)], writes=[("ps", bO)])

        def r_s4(g):
            bO = rbanks(g)[1]
            P.dve(lambda e, bO=bO, g=g: e.tensor_copy(out=oSb[g % 2], in_=ps[bO][:].rearrange("p (a e) -> p a e", a=4)),
                  reads=[("ps", bO)], writes=[oSk[g % 2]])
            finalize_a(P, A, oSb[g % 2], oSk[g % 2], 12 * (g % 2))

        def r_s5(g):
            finalize_b(P, C, A, oSb[g % 2], oSk[g % 2], 12 * (g % 2), ob4r[g % 2], ob4k[g % 2],
                       gate=lambda j, g=g: A.G2[:, 4 * g + j, 1, :], gate_keys=[("G2", 4 * g + j) for j in range(4)], split=True)

        def r_s6(g):
            finalize_b2(P, C, ob4r[g % 2], ob4k[g % 2])
            flush_oT(P, C, A, oT_dst(0, hh, slice(g * 512, (g + 1) * 512)), eng="act")

        rsched = {}
        for g in range(8):
            t0 = 4 * g + 3
            for dt, fns in ((0, (r_s1, r_s2)), (2, (r_s3, r_s4)), (4, (r_s5,)), (6, (r_s6,))):
                for fn in fns:
                    rsched.setdefault(t0 + dt, []).append((fn, g))
        for t in range(NT):
            b = 4 + t % 2
            T = t // 4
            ts_ = slice(t * 128, (t + 1) * 128)
            for c in range(8):
                P.pe(lambda e, b=b, c=c, ts_=ts_: e.matmul(ps[b][:], lhsT=A.hT[:, c, ts_], rhs=A.wbf[:, c, 384:896],
                                                          start=(c == 0), stop=(c == 7)),
                     reads=wkeys + hT_keys(T), writes=[("ps", b)])
            pv = lambda b: ps[b][:].rearrange("p (a b e) -> p a b e", a=2, b=2)
            P.act(lambda e, b=b, t=t: e.activation(out=A.V2[:, t, :, 0:128], in_=pv(b)[:, :, 0, :], func=AF.Copy),
                  reads=[("ps", b)], writes=[("V2", t)])
            sgb = t % 2
            P.act(lambda e, b=b, sgb=sgb: e.activation(out=A.sgt[sgb][:], in_=pv(b)[:, :, 1, :], func=AF.Sigmoid),
                  reads=[("ps", b)], writes=[("sgt", sgb)])
            P.dve(lambda e, b=b, t=t, sgb=sgb: e.tensor_tensor(out=A.G2[:, t, :, :], in0=pv(b)[:, :, 1, :], in1=A.sgt[sgb][:],
                                                              op=ALU.mult),
                  reads=[("ps", b), ("sgt", sgb)], writes=[("G2", t)])
            sl = t % 4
            for c in range(8):
                P.pe(lambda e, c=c, ts_=ts_, sl=sl: e.matmul(ps[6][:, sl * 64:(sl + 1) * 64], lhsT=A.hT[:, c, ts_],
                                                            rhs=A.wbf[:, c, 320:384], start=(c == 0), stop=(c == 7)),
                     reads=wkeys + hT_keys(T), writes=[("ps", 6)])
            if sl == 3:
                P.dve(lambda e, t=t, hh=hh: e.tensor_scalar(out=A.KrT[:, t - 3:t + 1, :],
                                                           in0=ps[6][:, 0:256].rearrange("p (s d) -> p s d", s=4),
                                                           scalar1=C.kdec[:, hh:hh + 1], scalar2=None, op0=ALU.mult),
                      reads=[("ps", 6), "kdec"], writes=[("KrT", t // 4)])
            if t >= 4 and t % 4 == 0:
                state_steps(t - 4, t)
            for fn, g in rsched.pop(t, []):
                fn(g)
        if hh + 1 < nheads:
            emit_weights(hh + 1)
        elif last_proj_done is not None:
            last_proj_done()
        state_steps(n_state_done[0], min(n_state_done[0] + 4, NT - 1))
        state_steps(n_state_done[0], NT - 1)
        for tt in sorted(rsched):
            for fn, g in rsched[tt]:
                fn(g)
        rsched.clear()
        if stop <= 4:
            continue
        islast = hh == nheads - 1
        tail_thunks[:] = attention_head(P, C, A, hh, neglam, oT_dst, step_hook if islast else None, defer_tail=not islast)
        tail_head[0] = hh
        if islast and head_done is not None:
            head_done(hh)


def attention_head(P, C, A, hh, neglam, oT_dst, step_hook=None, defer_tail=False):
    ps = C.ps
    steps = [(Tq, kt) for Tq in range(8) for kt in range(4 * Tq + 4)]

    def geom(Tq, kt):
        diag = kt >= 4 * Tq
        off = (kt - 4 * Tq) * 128 if diag else 0
        return diag, off, 512 - off

    def emit_qk(s):
        Tq, kt = steps[s]
        diag, off, N = geom(Tq, kt)
        q0 = Tq * 512
        sb = s % 2
        for j in range(2):
            bq = j * 2 + sb
            P.pe(lambda e, bq=bq, j=j, kt=kt, N=N, off=off, q0=q0, diag=diag: e.matmul(
                ps[bq][:, 0:N], lhsT=A.KT[j][:, kt * 128:(kt + 1) * 128],
                rhs=A.QT[j][:, q0 + off:q0 + 512], start=True, stop=not diag),
                reads=[("KT", j, kt // 4), ("KTaug", j), ("QTaug", j), ("QT", j, Tq)], writes=[("ps", bq)])
            if diag:
                P.pe(lambda e, bq=bq: e.matmul(ps[bq][:, 0:128], lhsT=C.ident[:], rhs=C.cmask[:], start=False, stop=True),
                     reads=["ident", "cmask"], writes=[("ps", bq)])
            P.act(lambda e, bq=bq, j=j, sb=sb, N=N: e.activation(out=A.PT[j][sb][:, 0:N], in_=ps[bq][:, 0:N], func=AF.Exp),
                  reads=[("ps", bq)], writes=[("PT", j, sb)])

    started = set()

    def emit_pv(s):
        Tq, kt = steps[s]
        diag, off, N = geom(Tq, kt)
        sb = s % 2
        if kt == 0:
            started.clear()
        for j in range(2):
            for qs in range(off // 128, 4):
                a = qs * 2 + j
                bank = 4 + a // 3
                col = (a % 3) * ACCW
                st = bank not in started
                started.add(bank)
                P.pe(lambda e, bank=bank, col=col, j=j, sb=sb, qs=qs, off=off, kt=kt, st=st, Tq=Tq: e.matmul(
                    ps[bank][:, col:col + 129], lhsT=A.PT[j][sb][:, qs * 128 - off:qs * 128 - off + 128],
                    rhs=A.V2[:, kt, 0, :], start=st, stop=(kt == 4 * Tq + qs), skip_group_check=True),
                    reads=[("PT", j, sb), ("V2", kt), "V2ones"], writes=[("ps", bank)])

    def emit_acc_copy(bk):
        na = 3 if bk < 2 else 2
        P.dve(lambda e, bk=bk, na=na: e.tensor_copy(
            out=A.accS[:, 3 * bk:3 * bk + na, :],
            in_=ps[4 + bk][:, 0:na * ACCW].rearrange("p (a w) -> p a w", w=ACCW)[:, :, 0:129]),
            reads=[("ps", 4 + bk)], writes=[("accS", bk)])

    def emit_final(Tq):
        q0 = Tq * 512
        allacc = [("accS", bk) for bk in range(3)]
        P.dve(lambda e: e.reciprocal(out=A.sm[:, 0:8], in_=A.accS[:, :, 128]), reads=allacc, writes=["rc8"])
        P.dve(lambda e: e.tensor_scalar(out=A.sm[:, 8:12], in0=A.sm[:, 1:8:2], scalar1=neglam, scalar2=None, op0=ALU.mult),
              reads=["rc8", "neglam"], writes=["rcl"])
        for qs in range(4):
            a0, a1 = qs * 2, qs * 2 + 1
            P.dve(lambda e, a1=a1, qs=qs: e.tensor_scalar(out=A.t1[:], in0=A.accS[:, a1, 0:128], scalar1=A.sm[:, 8 + qs:9 + qs],
                                                         scalar2=None, op0=ALU.mult),
                  reads=allacc + ["rcl"], writes=["t1"])
            P.dve(lambda e, a0=a0, qs=qs: e.scalar_tensor_tensor(out=A.oS[:, qs, :], in0=A.accS[:, a0, 0:128],
                                                                scalar=A.sm[:, a0:a0 + 1], in1=A.t1[:], op0=ALU.mult, op1=ALU.add),
                  reads=allacc + ["rc8", "t1"], writes=["oS"])
        finalize_a(P, A, A.oS, "oS", 24)

    def emit_final_b(Tq):
        finalize_b(P, C, A, A.oS, "oS", 24, A.ob4, "ob4", gate=lambda j, Tq=Tq: A.G2[:, Tq * 4 + j, 0, :],
                   gate_keys=[("G2", Tq * 4 + j) for j in range(4)], split=True)

    def emit_final_c(Tq):
        q0 = Tq * 512
        finalize_b2(P, C, A.ob4, "ob4")
        flush_oT(P, C, A, oT_dst(1, hh, slice(q0, q0 + 512)))

    emit_qk(0)
    pending = []
    for s in range(len(steps)):
        if s + 1 < len(steps):
            emit_qk(s + 1)
        emit_pv(s)
        if step_hook is not None:
            step_hook(s)
        Tq, kt = steps[s]
        while pending and pending[0][0] <= s:
            _, fnp, tq = pending.pop(0)
            fnp(tq)
        if kt >= 4 * Tq + 1:
            emit_acc_copy(kt - 4 * Tq - 1)
        if kt == 4 * Tq + 3:
            emit_final(Tq)
            pending.append((s + 6, emit_final_b, Tq))
            pending.append((s + 8, emit_final_c, Tq))
    tail = [(lambda fnp=fnp, tq=tq: fnp(tq)) for _, fnp, tq in pending]
    if defer_tail:
        return tail
    for th in tail:
        th()
    return []


def finalize_a(P, A, oS, oSkey, c0):
    for j in range(4):
        P.dve(lambda e, j=j: e.scalar_tensor_tensor(out=A.sqj[:], in0=oS[:, j, :], scalar=1.0, in1=oS[:, j, :],
                                                    op0=ALU.mult, op1=ALU.mult, accum_out=A.sm2[:, c0 + j:c0 + j + 1]),
              reads=[oSkey], writes=[("ss4", c0, j)])


def finalize_b(P, C, A, oS, oSkey, c0, ob4, ob4key, gate, gate_keys, split=False):
    P.act(lambda e: e.activation(out=A.sm2[:, c0 + 4:c0 + 8], in_=A.sm2[:, c0:c0 + 4], func=AF.Ln, scale=1.0 / 128, bias=EPS),
          reads=[("ss4", c0, j) for j in range(4)], writes=[("ln4", c0)])
    P.act(lambda e: e.activation(out=A.sm2[:, c0 + 8:c0 + 12], in_=A.sm2[:, c0 + 4:c0 + 8], func=AF.Exp, scale=-0.5),
          reads=[("ln4", c0)], writes=[("rr4", c0)])
    for j in range(4):
        P.dve(lambda e, j=j: e.scalar_tensor_tensor(out=ob4[:, j, :], in0=oS[:, j, :], scalar=A.sm2[:, c0 + 8 + j:c0 + 9 + j],
                                                    in1=gate(j), op0=ALU.mult, op1=ALU.mult),
              reads=[oSkey, ("rr4", c0), gate_keys[j]], writes=[ob4key])
    if not split:
        finalize_b2(P, C, ob4, ob4key)


def finalize_b2(P, C, ob4, ob4key):
    for j in range(4):
        P.pe(lambda e, j=j: e.transpose(out=C.psT[:, j * 128:(j + 1) * 128], in_=ob4[:, j, :], identity=C.ident[:]),
             reads=[ob4key, "ident"], writes=["psT"])


def flush_oT(P, C, A, dst_dram, eng="dve"):
    sb = A.n_oTst % 2
    A.n_oTst += 1
    if eng == "dve":
        P.dve(lambda e: e.tensor_copy(out=A.oTst[sb][:], in_=C.psT[:, 0:512]),
              reads=["psT"], writes=[("oTst", sb)])
    else:
        P.act(lambda e: e.activation(out=A.oTst[sb][:], in_=C.psT[:, 0:512], func=AF.Copy),
              reads=["psT"], writes=[("oTst", sb)])
    k = ("oT_out", A.n_oTst)
    A.out_keys.append(k)
    P.dma("pool", lambda e: e.dma_start(out=dst_dram, in_=A.oTst[sb][:]), reads=[("oTst", sb)], writes=[k])


def host_weights_B(inp, l):
    w = {}
    w["wr"] = np.ascontiguousarray(inp["ret_w_o"][l])
    w["wd"] = np.ascontiguousarray(inp["diff_w_o"][l])
    w["wmr"] = np.ascontiguousarray(inp["w_in"][l][:, 7168:8192])
    w["wmd"] = np.ascontiguousarray(inp["w_in"][l][:, 8192:9216])
    w["wo"] = np.ascontiguousarray(inp["w_out"][l])
    spb = np.empty((128, 24), np.float32)
    spb[:, 0:8] = inp["norm_g"][l].reshape(8, 128).T
    spb[:, 8:16] = inp["ret_norm_g"][l].reshape(8, 128).T
    spb[:, 16:24] = inp["diff_sub_norm_g"][l].reshape(8, 128).T
    w["spb"] = spb
    return w


def alloc_B(nc, hT_own=None, al=None):
    al = al or nc.alloc_sbuf_tensor
    B = Ctx()
    B.W = {nm: al(f"B_{nm}", [128, 8, D], BF16) for nm in ("wo", "wr", "wd", "wmr", "wmd")}
    B.stg = [al(f"B_stg{i}", [128, 512], F32) for i in range(4)]
    B.spb = al("B_spb", [128, 24], F32)
    B.hT = hT_own if hT_own is not None else al("B_hT", [128, 8, 2048], BF16)
    B.oT = [al(f"B_oT{i}", [128, 16, 512], BF16) for i in range(2)]
    B.mT = al("B_mT", [128, 8, 512], BF16)
    B.sg = [al(f"B_sg{i}", [128, 512], F32) for i in range(2)]
    B.m1 = al("B_m1", [128, 512], F32)
    B.m2 = al("B_m2", [128, 512], F32)
    B.n_stg = 0
    B.out_keys = []
    return B


def stage_B_weight_thunks(P, B, l, wB, engines=("act", "dve", "act", "pool", "act", "dve")):
    lam_init = lam_init_of(l)
    out = []

    def head():
        P.dma("sp", lambda e: e.dma_start(out=B.spb[:], in_=wB["spb"]), writes=["spb"])
        P.dve(lambda e: e.tensor_scalar(out=B.spb[:, 16:24], in0=B.spb[:, 16:24], scalar1=1.0 - lam_init, scalar2=None, op0=ALU.mult),
              reads=["spb"], writes=["spb"])
    out.append(head)
    order = [(nm, g0, half) for half in range(2) for nm, g0 in (("wmr", 0), ("wmd", 0), ("wr", 8), ("wd", 16))]
    order += [("wo", None, 0), ("wo", None, 1)]
    pieces = [(nm, g0, half, c) for nm, g0, half in order for c in range(8)]

    def piece(i):
        nm, gcol0, half, c = pieces[i]
        hs = slice(half * 512, (half + 1) * 512)
        sb = i % 4
        eng = engines[i % len(engines)]
        P.dma("sp", lambda e: e.dma_start(out=B.stg[sb][:], in_=wB[nm][c * 128:(c + 1) * 128, hs]), writes=[("Bstg", sb)])
        dst = B.W[nm][:, c, hs]
        if gcol0 is None:
            if eng == "act":
                fn = lambda e: e.activation(out=dst, in_=B.stg[sb][:], func=AF.Copy)
            else:
                fn = lambda e: e.tensor_copy(out=dst, in_=B.stg[sb][:])
            P.add(eng, fn, reads=[("Bstg", sb)], writes=[("BW", nm, c, half)])
        else:
            col = B.spb[:, gcol0 + c:gcol0 + c + 1]
            if eng == "act":
                fn = lambda e: e.activation(out=dst, in_=B.stg[sb][:], func=AF.Copy, scale=col)
            elif eng == "dve":
                fn = lambda e: e.tensor_scalar(out=dst, in0=B.stg[sb][:], scalar1=col, scalar2=None, op0=ALU.mult)
            else:
                fn = lambda e: e.tensor_scalar(out=dst, in0=B.stg[sb][:], scalar1=col, scalar2=1.0, op0=ALU.mult, op1=ALU.mult)
            P.add(eng, fn, reads=[("Bstg", sb), "spb"], writes=[("BW", nm, c, half)])

    for i in range(len(pieces)):
        out.append(lambda i=i: piece(i))
    return out


def stage_B(P, nc, C, B, N, l, wB, oTB_dram, hT_dram, x_dram, xout_dram, hTn_dram, last, oT_load=None, tile_done=None, after_last_jloop=None):
    ps = C.ps
    lam_init = lam_init_of(l)
    if not getattr(B, "weights_done", False):
        for th in stage_B_weight_thunks(P, B, l, wB):
            th()
    B.weights_done = False
    def load_hT_own(T):
        P.dma("sp", lambda e, T=T: e.dma_start(out=B.hT[:, :, T * 512:(T + 1) * 512], in_=hT_dram(e, T)), writes=[("BhT", T)])

    if callable(hT_dram):
        if oT_load is None:
            for T in range(4):
                load_hT_own(T)
    elif hT_dram is not None:
        for c in range(8):
            P.dma("sp", lambda e, c=c: e.dma_start(out=B.hT[:, c, :], in_=hT_dram[c * 128:(c + 1) * 128, :]),
                  writes=[("BhT", T) for T in range(4)])
    wk = lambda nm, half=None: [("BW", nm, c, h) for c in range(8) for h in ((0, 1) if half is None else (half,))]
    carry = []
    for T in range(4):
        tok = slice(T * 512, (T + 1) * 512)
        ob = T % 2
        if oT_load is not None:
            if T == 0:
                oT_load(0, 0)
                load_hT_own(0)
                for TT in range(1, 4):
                    load_hT_own(TT)
            if T + 1 < 4:
                oT_load(T + 1, (T + 1) % 2)
        else:
          for br in range(2):
            P.dma("sp", lambda e, br=br, ob=ob, tok=tok: e.dma_start(
                out=B.oT[ob][:, br * 8:(br + 1) * 8, :], in_=oTB_dram[br, :, :, tok].rearrange("h p t -> p h t")),
                writes=[("BoT", ob, br)])
        for j in range(8):
            cols = slice(j * 128, (j + 1) * 128)
            for bi, (nm, src) in enumerate((("wr", 0), ("wd", 1))):
                for hd in range(8):
                    P.pe(lambda e, bi=bi, nm=nm, src=src, hd=hd, cols=cols, ob=ob: e.matmul(
                        ps[bi][:], lhsT=B.W[nm][:, hd, cols], rhs=B.oT[ob][:, src * 8 + hd, :], start=(hd == 0), stop=(hd == 7)),
                        reads=wk(nm, j // 4) + [("BoT", ob, src)], writes=[("ps", bi)])
            for bi, nm in ((2, "wmr"), (3, "wmd")):
                for c in range(8):
                    P.pe(lambda e, bi=bi, nm=nm, c=c, cols=cols, tok=tok: e.matmul(
                        ps[bi][:], lhsT=B.W[nm][:, c, cols], rhs=B.hT[:, c, tok], start=(c == 0), stop=(c == 7)),
                        reads=wk(nm, j // 4) + [("BhT", T)], writes=[("ps", bi)])
            if j == 1 and carry:
                for fnc in carry:
                    fnc()
                carry[:] = []
            P.act(lambda e: e.activation(out=B.sg[0][:], in_=ps[2][:], func=AF.Sigmoid), reads=[("ps", 2)], writes=[("Bsg", 0)])
            P.act(lambda e: e.activation(out=B.sg[1][:], in_=ps[3][:], func=AF.Sigmoid), reads=[("ps", 3)], writes=[("Bsg", 1)])
            P.dve(lambda e: e.tensor_tensor(out=B.m1[:], in0=ps[0][:], in1=B.sg[0][:], op=ALU.mult),
                  reads=[("ps", 0), ("Bsg", 0)], writes=["Bm1"])
            P.dve(lambda e: e.tensor_tensor(out=B.m2[:], in0=ps[1][:], in1=B.sg[1][:], op=ALU.mult),
                  reads=[("ps", 1), ("Bsg", 1)], writes=["Bm2"])
            P.pool(lambda e, j=j: e.tensor_tensor(out=B.mT[:, j, :], in0=B.m1[:], in1=B.m2[:], op=ALU.add),
                   reads=["Bm1", "Bm2"], writes=[("BmT", j)])
        pre_thunks = []
        if T == 3 and after_last_jloop is not None:
            pre_thunks = after_last_jloop() or []
        pend = []
        for tsub in range(4):
            t = T * 4 + tsub
            xb = t % 2
            banks = (4, 5) if tsub % 2 == 0 else (6, 3)
            P.dma("sp", lambda e, t=t, xb=xb: e.dma_start(out=N.xt[xb][:], in_=(x_dram(e, t) if callable(x_dram) else x_dram[t * 128:(t + 1) * 128, :])),
                  writes=[("N", "xt", xb)])
            for _ in range(3):
                if pre_thunks:
                    pre_thunks.pop(0)()
            for chh in range(2):
                bo = banks[chh]
                for j in range(8):
                    P.pe(lambda e, bo=bo, j=j, tsub=tsub, chh=chh: e.matmul(
                        ps[bo][:], lhsT=B.mT[:, j, tsub * 128:(tsub + 1) * 128], rhs=B.W["wo"][:, j, chh * 512:(chh + 1) * 512],
                        start=(j == 0), stop=(j == 7)),
                        reads=wk("wo", chh) + [("BmT", jj) for jj in range(8)], writes=[("ps", bo)])
            for fnp in pend:
                fnp()
            pend = []
            for chh in range(2):
                bo = banks[chh]
                P.dve(lambda e, bo=bo, xb=xb, chh=chh: e.tensor_tensor(out=N.xt[xb][:, chh * 512:(chh + 1) * 512], in0=ps[bo][:],
                                                                      in1=N.xt[xb][:, chh * 512:(chh + 1) * 512], op=ALU.add),
                      reads=[("ps", bo), ("N", "xt", xb)], writes=[("N", "xt", xb)])
            k = ("xout", l, t)
            B.out_keys.append(k)
            P.dma("pool", lambda e, t=t, xb=xb: e.dma_start(out=xout_dram[t * 128:(t + 1) * 128, :], in_=N.xt[xb][:]),
                  reads=[("N", "xt", xb)], writes=[k])
            if not last:
                norm_tile_a(P, C, N, xb, ("N", "xt", xb))
                pend.append(lambda xb=xb, t=t, T=T: norm_tile_t(P, C, N, xb, B.hT[:, :, t * 128:(t + 1) * 128], ("BhT", T)))
        while pre_thunks:
            pre_thunks.pop(0)()

        def tile_tail(pend=pend, T=T, tok=tok):
            for fnp in pend:
                fnp()
            if not last and callable(hTn_dram):
                k = ("hTn", l, T)
                B.out_keys.append(k)
                P.dma("pool", lambda e: e.dma_start(out=hTn_dram(T), in_=B.hT[:, :, tok]), reads=[("BhT", T)], writes=[k])
                if tile_done is not None:
                    tile_done(T, [k])

        if callable(hTn_dram) or last:
            if T < 3:
                carry.append(tile_tail)
            else:
                tile_tail()
            continue
        for fnp in pend:
            fnp()
        if not last:
            if callable(hTn_dram):
                pass
            else:
                k = ("hTn", l, T)
                B.out_keys.append(k)
                P.dma("pool", lambda e, tok=tok: e.dma_start(out=hTn_dram.rearrange("(c p) t -> p c t", p=128)[:, :, tok],
                                                            in_=B.hT[:, :, tok]),
                      reads=[("BhT", T)], writes=[k])


from concourse.bass_utils import run_bass_kernel_spmd

NCORES = 8
PAIRS = [[0, 1], [2, 3], [4, 5], [6, 7]]


def _dram_in(nc, name, arr):
    dt = {np.dtype(np.float32): F32, np.dtype(bf16): BF16}[arr.dtype]
    return nc.dram_tensor(name, list(arr.shape), dt, kind="ExternalInput").ap()


class Arena:
    def __init__(self, nc, nbytes):
        self.nbytes = nbytes
        self.t = nc.alloc_sbuf_tensor("arena", [128, nbytes // 2], BF16)
        self.off = 0
        self.peak = 0

    def reset(self):
        self.off = 0

    def alloc(self, name, shape, dtype):
        n = 1
        for d in shape[1:]:
            n *= d
        esz = 4 if dtype == F32 else 2
        size = (n * esz + 31) // 32 * 32
        assert self.off + size <= self.nbytes, (name, self.off, size, self.nbytes)
        ap = self.t[0:shape[0], self.off // 2:(self.off + n * esz) // 2]
        if dtype == F32:
            ap = ap.bitcast(F32)
        if len(shape) > 2:
            names = " ".join(f"d{i}" for i in range(len(shape) - 1))
            kw = {f"d{i}": shape[i + 1] for i in range(len(shape) - 2)}
            ap = ap.rearrange(f"p ({names}) -> p {names}", **kw)
        self.off += size
        self.peak = max(self.peak, self.off)
        return ap


def build_fused(sample, stop_stage=99, nheads=NH):
    nc = bass.Bass("TRN2", target_bir_lowering=False)
    cd = {k: _dram_in(nc, "i_" + k, v) for k, v in sample.items()}
    out = nc.dram_tensor("out", [2048, D], F32, kind="ExternalOutput").ap()
    hT_in = [[nc.dram_tensor(f"hT_in{l}_{T}", [128, 4096], BF16, kind="Internal").ap() for T in range(4)] for l in range(2)]
    hT_ag = [[nc.dram_tensor(f"hT_ag{l}_{T}", [256, 4096], BF16, kind="Internal").ap() for T in range(4)] for l in range(2)]
    oT_loc = [[nc.dram_tensor(f"oT_loc{l}_{h}", [256, S], BF16, kind="Internal").ap() for h in range(NH)] for l in range(2)]
    oT_ag = [[nc.dram_tensor(f"oT_ag{l}_{h}", [512, S], BF16, kind="Internal").ap() for h in range(NH)] for l in range(2)]
    x_mid = nc.dram_tensor("x_mid", [2048, D], F32, kind="Internal").ap()
    hT_all0 = nc.dram_tensor("hT_all0", [8 * 128, 4096], BF16, kind="Internal").ap()

    P = Prog(nc, n_dma_sems=16)
    dyn = {}
    C = alloc_common(P, nc, cd)
    AR = Arena(nc, 196 * 1024)
    A = alloc_A(nc, al=AR.alloc)
    peakA = AR.off
    dead_bytes = 65536 + 8 * WA_COLS * 2 + 2 * WA_COLS * 4 + 1088 + 2 * 1024 + 2 * 1024
    AR.reset()
    B = alloc_B(nc, al=AR.alloc)
    N = alloc_N(nc, al=AR.alloc)
    peakB = AR.off
    assert 5 * 16384 + 4 * 2048 + 96 <= dead_bytes, dead_bytes

    def ag_hT(l, T, keys):
        P.coll(lambda e: e.collective_compute("AllGather", ALU.bypass, replica_groups=PAIRS, ins=[hT_in[l][T]], outs=[hT_ag[l][T]]),
               reads=keys, writes=[("hT_ag", l, T)])

    def hTn_dst(l):
        return lambda T: hT_in[l][T].rearrange("p (c t) -> p c t", c=8)

    def hT_own(l):
        if l == 0:
            def own0(e, T):
                if "hoffs" not in dyn:
                    dyn["hoffs"] = [rpar(e) * 512 + TT * 128 for TT in range(4)]
                return hT_all0[bass.ds(dyn["hoffs"][T], 128), :]
            return own0
        return lambda e, T: hT_in[l][T].rearrange("p (c t) -> p c t", c=8)

    def hT_full(l):
        return lambda r, T: hT_ag[l][T][r * 128:(r + 1) * 128, :].rearrange("p (c t) -> p c t", c=8)

    def rpar(e):
        if "r" not in dyn:
            dyn["r"] = e.partition_id() % 2
        return dyn["r"]

    def roff(e, mult):
        if "r" not in dyn:
            dyn["r"] = e.partition_id() % 2
        if mult not in dyn:
            dyn[mult] = e.snap(e.to_reg(dyn["r"] * mult))
        return dyn[mult]

    def dyn_view(e, name, make):
        if name not in dyn:
            dyn[name] = make()
        return dyn[name]

    N0 = Ctx()
    xt4 = B.oT[0][:].rearrange("p h t -> p (h t)").bitcast(F32)
    xn4 = B.oT[1][:].rearrange("p h t -> p (h t)")
    N0.xt = [xt4[:, i * 1024:(i + 1) * 1024] for i in range(4)]
    N0.xn = [xn4[:, i * 1024:(i + 1) * 1024] for i in range(4)]
    N0.sq = N.sq
    N0.ss = [B.m1[:, i:i + 1] for i in range(4)]
    N0.rs = [B.m1[:, 8 + i:9 + i] for i in range(4)]
    def n_store(Tg):
        P.dma("pool", lambda e, Tg=Tg: e.dma_start(out=hT_all0[Tg * 128:(Tg + 1) * 128, :].rearrange("p (c t) -> p c t", c=8),
                                                   in_=A.hT[:, :, Tg * 512:(Tg + 1) * 512]),
              reads=[("hT", Tg)], writes=[("hT_all0", Tg)])

    for t in range(32 + 2):
        if t < 32:
            b = t % 4
            P.dma("sp", lambda e, t=t, b=b: e.dma_start(out=N0.xt[b], in_=cd["x"][t * 128:(t + 1) * 128, :]),
                  writes=[("N", "xt", b)])
            norm_tile_a(P, C, N0, b, ("N", "xt", b), sq_eng=("dve" if t % 2 else "act"), stats_only=True)
        if 0 <= t - 1 < 32:
            norm_tile_scale(P, N0, (t - 1) % 4, ("N", "xt", (t - 1) % 4))
        if 0 <= t - 2 < 32:
            u = t - 2
            norm_tile_t(P, C, N0, u % 4, A.hT[:, :, u * 128:(u + 1) * 128], ("hT", u // 4), eng=("act" if u % 2 else "dve"))
            if u % 4 == 3:
                n_store(u // 4)
    stage_A_pre(P, A, cd["wA0"], cd["sp0"])
    P.fence()

    def finish():
        P.fence()
        stats = P.emit()
        return nc, stats, (peakA, peakB)

    A.nheads = nheads
    stage_no = 1
    for l in range(2):
        if stop_stage <= stage_no:
            return finish()
        stage_no += 2
        last = l == 1
        hT_src = hT_full(l) if l > 0 else None

        def oT_dst(br, hh, tsl, l=l):
            return oT_loc[l][hh][br * 128:(br + 1) * 128, tsl]

        nk0 = [0]

        def head_done(hh, l=l):
            keys = A.out_keys[nk0[0]:]
            nk0[0] = len(A.out_keys)
            P.coll(lambda e: e.collective_compute("AllGather", ALU.bypass, replica_groups=PAIRS,
                                                  ins=[oT_loc[l][hh]], outs=[oT_ag[l][hh]]),
                   reads=keys, writes=[("oT_ag", l, hh)])

        nk0[0] = len(A.out_keys)
        wB = {nm: cd[f"{nm}{l}"] for nm in ("wr", "wd", "wmr", "wmd", "wo", "spb")}
        thunks = stage_B_weight_thunks(P, B, l, wB, engines=("pool",))
        tpos = [0]

        def last_proj_done():
            P.barrier_on(A.dead_keys, ("sp", "pool", "dve"))
            thunks[0]()
            tpos[0] = 1

        def step_hook(s):
            n = 1 if s % 2 == 0 else 0
            if s >= 20:
                n = 1
            for _ in range(n):
                if tpos[0] < len(thunks):
                    thunks[tpos[0]]()
                    tpos[0] += 1

        stage_A(P, nc, C, A, l, None, hT_src, cd[f"wA{l}"], cd[f"sp{l}"], cd["augk"], cd["augq"], None, True,
                oT_dst=oT_dst, head_done=head_done, last_proj_done=last_proj_done, step_hook=step_hook, pre_done=True)
        while tpos[0] < len(thunks):
            thunks[tpos[0]]()
            tpos[0] += 1
        B.weights_done = True
        P.fence()
        if stop_stage <= stage_no - 1:
            return finish()

        def oT_load(T, ob, l=l):
            for hh in range(NH):
                for s in range(2):
                    for br in range(2):
                        def fn(e, hh=hh, s=s, br=br, T=T, ob=ob):
                            if "offs" not in dyn:
                                dyn["offs"] = [rpar(e) * 2048 + TT * 512 for TT in range(4)]
                            src = oT_ag[l][hh][s * 256 + br * 128:s * 256 + (br + 1) * 128, bass.ds(dyn["offs"][T], 512)]
                            return e.dma_start(out=B.oT[ob][:, br * 8 + 4 * s + hh, :], in_=src)
                        P.dma("sp", fn, reads=[("oT_ag", l, hh)], writes=[("BoT", ob, br)])

        def after_last_jloop(l=l):
            P.barrier_on([("BW", "wmd", c, h) for c in range(8) for h in range(2)] + [("Bstg", i) for i in range(4)], ("sp", "pool"))
            return stage_A_pre_thunks(P, A, cd[f"wA{l + 1}"], cd[f"sp{l + 1}"])

        x_src = cd["x_own"] if l == 0 else x_mid
        x_dst = x_mid if l == 0 else out
        nb0 = len(B.out_keys)
        stage_B(P, nc, C, B, N, l, wB, None, hT_own(l), x_src, x_dst, None if last else hTn_dst(l + 1), last, oT_load=oT_load,
                tile_done=(None if last else (lambda T, keys, l=l: ag_hT(l + 1, T, keys))),
                after_last_jloop=(None if last else after_last_jloop))
        P.fence()
    stats = P.emit()
    return nc, stats, (peakA, peakB)


def host_inputs(inp):
    consts = [host_consts(g) for g in range(2)]
    x = inp["x"].astype(np.float32, copy=False)
    wbs = [host_weights_B(inp, l) for l in range(2)]
    was = [[host_weights_A(inp, l, g) for g in range(2)] for l in range(2)]
    maps = []
    for c in range(NCORES):
        b, g = c // 2, c % 2
        m = dict(consts[g])
        m["x"] = x[b]
        m["x_own"] = np.ascontiguousarray(x[b, g * 2048:(g + 1) * 2048])
        for l in range(2):
            m[f"wA{l}"], m[f"sp{l}"] = was[l][g]
            for nm, v in wbs[l].items():
                m[f"{nm}{l}"] = v
        maps.append(m)
    return maps


def kernel(**inputs):
    inp = {k: np.asarray(v) for k, v in inputs.items()}
    maps = host_inputs(inp)
    nc, stats, peaks = build_fused(maps[0])
    res = run_bass_kernel_spmd(nc, [{"i_" + k: v for k, v in m.items()} for m in maps], core_ids=list(range(NCORES)))
    out = np.empty((4, S, D), np.float32)
    for c in range(NCORES):
        out[c // 2, (c % 2) * 2048:(c % 2 + 1) * 2048] = res.results[c]["out"]
    return out
```
